# Optimizing a Trainium2 kernel written in Bass

```python
import math
import jax
import jax.numpy as jnp
from jax import lax
import numpy as np

D_MODEL = 1024
BATCH = 16
SEQ = 2048
DEPTH = 1

CHUNK = 64
Q_BLOCK = 128
DA_HEADS = 8
DA_HEAD_DIM = D_MODEL // (2 * DA_HEADS)
DA_WIDTH = DA_HEADS * 2 * DA_HEAD_DIM
ML_HEADS = 8
ML_WIDTH = D_MODEL
ML_HEAD_DIM = ML_WIDTH // ML_HEADS
CONV_K = 4
N_BUCKETS = 32
MAX_DISTANCE = 128
NORM_EPS = 1e-6
SUBLN_EPS = 1e-5
HEAD_LN_EPS = 1e-5
NEG_INF = -1e30
IN_SIZES = (DA_WIDTH, DA_WIDTH, DA_WIDTH, DA_WIDTH,
            ML_WIDTH, ML_WIDTH, ML_WIDTH, ML_WIDTH, ML_WIDTH,
            ML_HEADS, ML_HEADS, D_MODEL, D_MODEL)
D_IN = 4 * DA_WIDTH + 5 * ML_WIDTH + 2 * ML_HEADS + 2 * D_MODEL

kernel_name = "hybrid_diffattn_mlstm_gated_block"


def _rmsnorm(x, w, eps=NORM_EPS):
    xf = x.astype(jnp.float32)
    y = xf * lax.rsqrt(jnp.mean(xf * xf, axis=-1, keepdims=True) + eps)
    return (y * w.astype(jnp.float32)).astype(x.dtype)


def _split_columns(w):
    offsets = [int(o) for o in np.cumsum(IN_SIZES)[:-1]]
    return jnp.split(w, offsets, axis=-1)


def _rel_bucket(rel):
    nb = N_BUCKETS // 2
    max_exact = nb // 2
    bucket = jnp.where(rel > 0, nb, 0)
    n = jnp.abs(rel)
    nf = jnp.maximum(n, 1).astype(jnp.float32)
    large = max_exact + (jnp.log(nf / max_exact) / math.log(MAX_DISTANCE / max_exact)
                         * (nb - max_exact)).astype(jnp.int32)
    large = jnp.minimum(large, nb - 1)
    return bucket + jnp.where(n < max_exact, n, large)


def _diff_attention(q, k, v, lam_vecs, subln_w, rel_bias, lambda_init):
    b, s, _ = q.shape
    out_dtype = q.dtype
    q = (q.reshape(b, s, DA_HEADS, 2, DA_HEAD_DIM) * DA_HEAD_DIM ** -0.5).transpose(3, 0, 2, 1, 4)
    k = k.reshape(b, s, DA_HEADS, 2, DA_HEAD_DIM).transpose(3, 0, 2, 1, 4)
    v = v.reshape(b, s, DA_HEADS, 2 * DA_HEAD_DIM).transpose(0, 2, 1, 3).astype(jnp.float32)
    lf = lam_vecs.astype(jnp.float32)
    lam = jnp.exp(jnp.sum(lf[0] * lf[1])) - jnp.exp(jnp.sum(lf[2] * lf[3])) + lambda_init
    kpos = jnp.arange(s)

    def block(i):
        q0 = i * Q_BLOCK
        qb = lax.dynamic_slice_in_dim(q, q0, Q_BLOCK, axis=3)
        qpos = q0 + jnp.arange(Q_BLOCK)
        bias = rel_bias[_rel_bucket(kpos[None, :] - qpos[:, None])]
        bias = bias.astype(jnp.float32).transpose(2, 0, 1)
        mask = (kpos[None, :] // CHUNK) <= (qpos[:, None] // CHUNK)
        logits = jnp.einsum("pbhqd,pbhkd->pbhqk", qb, k).astype(jnp.float32) + bias
        probs = jax.nn.softmax(jnp.where(mask, logits, NEG_INF), axis=-1)
        attn = probs[0] - lam * probs[1]
        return jnp.einsum("bhqk,bhkv->bhqv", attn, v)

    out = lax.map(block, jnp.arange(s // Q_BLOCK))
    out = out.transpose(1, 2, 0, 3, 4).reshape(b, DA_HEADS, s, 2 * DA_HEAD_DIM)
    out = _rmsnorm(out, subln_w, SUBLN_EPS) * (1.0 - lambda_init)
    return out.transpose(0, 2, 1, 3).reshape(b, s, DA_WIDTH).astype(out_dtype)


def _causal_conv(u, w, bias):
    s = u.shape[1]
    up = jnp.pad(u, ((0, 0), (CONV_K - 1, 0), (0, 0)))
    out = bias
    for j in range(CONV_K):
        out = out + w[j] * up[:, j:j + s, :]
    return out


def _mlstm_chunk(carry, inp):
    c_state, n_state, m_state = carry
    q, k, v, ig, lf = inp
    q = q.astype(jnp.float32)
    k = k.astype(jnp.float32)
    v = v.astype(jnp.float32)
    b = jnp.cumsum(lf, axis=-1)
    causal = jnp.arange(CHUNK)[:, None] >= jnp.arange(CHUNK)[None, :]
    d_log = jnp.where(causal, b[..., :, None] - b[..., None, :] + ig[..., None, :], NEG_INF)
    inter = b + m_state[..., None]
    m_t = jnp.maximum(inter, jnp.max(d_log, axis=-1))
    w_intra = jnp.exp(d_log - m_t[..., None])
    w_inter = jnp.exp(inter - m_t)
    scores = jnp.einsum("bhtd,bhsd->bhts", q, k) * w_intra
    num = (jnp.einsum("bhts,bhsv->bhtv", scores, v)
           + w_inter[..., None] * jnp.einsum("bhtd,bhdv->bhtv", q, c_state))
    den = jnp.sum(scores, axis=-1) + w_inter * jnp.einsum("bhtd,bhd->bht", q, n_state)
    h = num / jnp.maximum(jnp.abs(den), jnp.exp(-m_t))[..., None]
    b_last = b[..., -1]
    w_log = b_last[..., None] - b + ig
    m_new = jnp.maximum(b_last + m_state, jnp.max(w_log, axis=-1))
    w_s = jnp.exp(w_log - m_new[..., None])
    decay = jnp.exp(b_last + m_state - m_new)
    c_new = decay[..., None, None] * c_state + jnp.einsum("bhs,bhsd,bhsv->bhdv", w_s, k, v)
    n_new = decay[..., None] * n_state + jnp.einsum("bhs,bhsd->bhd", w_s, k)
    return (c_new, n_new, m_new), h


def _mlstm(q, k, v, ig, fg, o, conv_w, conv_b, b_if, mh_w):
    b, s, _ = q.shape
    nc = s // CHUNK
    out_dtype = v.dtype
    q = jax.nn.silu(_causal_conv(q, conv_w[0], conv_b[0]))
    k = jax.nn.silu(_causal_conv(k, conv_w[1], conv_b[1])) * ML_HEAD_DIM ** -0.5

    def heads_to_chunks(u):
        return u.reshape(b, nc, CHUNK, ML_HEADS, ML_HEAD_DIM).transpose(1, 0, 3, 2, 4)

    def gates_to_chunks(g):
        return g.reshape(b, nc, CHUNK, ML_HEADS).transpose(1, 0, 3, 2)

    ig_c = gates_to_chunks((ig + b_if[0]).astype(jnp.float32))
    lf_c = gates_to_chunks(jax.nn.log_sigmoid((fg + b_if[1]).astype(jnp.float32)))
    init = (jnp.zeros((b, ML_HEADS, ML_HEAD_DIM, ML_HEAD_DIM), jnp.float32),
            jnp.zeros((b, ML_HEADS, ML_HEAD_DIM), jnp.float32),
            jnp.zeros((b, ML_HEADS), jnp.float32))
    _, h = lax.scan(_mlstm_chunk, init,
                    (heads_to_chunks(q), heads_to_chunks(k), heads_to_chunks(v), ig_c, lf_c))
    h = h.transpose(1, 0, 3, 2, 4).reshape(b, s, ML_HEADS, ML_HEAD_DIM)
    mu = jnp.mean(h, axis=-1, keepdims=True)
    var = jnp.mean(jnp.square(h - mu), axis=-1, keepdims=True)
    hn = (h - mu) * lax.rsqrt(var + HEAD_LN_EPS) * mh_w.astype(jnp.float32).reshape(ML_HEADS, ML_HEAD_DIM)
    return (jax.nn.sigmoid(o.astype(jnp.float32)) * hn.reshape(b, s, ML_WIDTH)).astype(out_dtype)


def _normal(key, shape, scale):
    return scale * jax.random.normal(key, shape, jnp.float32)


def setup_inputs(seed: int = 0) -> dict:
    key = jax.random.key(seed)
    ks = jax.random.split(key, 18)
    b_if = jnp.stack([
        _normal(ks[8], (DEPTH, ML_HEADS), 0.1),
        jnp.linspace(3.0, 6.0, ML_HEADS, dtype=jnp.float32)[None, :] + _normal(ks[9], (DEPTH, ML_HEADS), 0.1),
    ], axis=1)
    return {
        "x": _normal(ks[0], (BATCH, SEQ, D_MODEL), 1.0),
        "norm_w": 1.0 + _normal(ks[1], (DEPTH, D_MODEL), 0.02),
        "w_in": _normal(ks[2], (DEPTH, D_MODEL, D_IN), D_MODEL ** -0.5),
        "lam": _normal(ks[3], (DEPTH, 4, DA_HEAD_DIM), 0.1),
        "subln_w": 1.0 + _normal(ks[4], (DEPTH, 2 * DA_HEAD_DIM), 0.02),
        "rel_bias": _normal(ks[5], (N_BUCKETS, DA_HEADS), 0.1),
        "conv_w": _normal(ks[6], (DEPTH, 2, CONV_K, ML_WIDTH), CONV_K ** -0.5),
        "conv_b": _normal(ks[7], (DEPTH, 2, ML_WIDTH), 0.02),
        "b_if": b_if,
        "mh_w": 1.0 + _normal(ks[10], (DEPTH, ML_WIDTH), 0.02),
        "b_gate": _normal(ks[11], (DEPTH, 2, D_MODEL), 0.02),
        "w_pa": _normal(ks[12], (DEPTH, DA_WIDTH, D_MODEL), DA_WIDTH ** -0.5),
        "w_pm": _normal(ks[13], (DEPTH, ML_WIDTH, D_MODEL), ML_WIDTH ** -0.5),
        "w_out": _normal(ks[14], (DEPTH, D_MODEL, D_MODEL), D_MODEL ** -0.5),
        "norm_final": 1.0 + _normal(ks[15], (D_MODEL,), 0.02),
    }


def reference(x, norm_w, w_in, lam, subln_w, rel_bias, conv_w, conv_b, b_if, mh_w,
              b_gate, w_pa, w_pm, w_out, norm_final):
    h = x
    for layer in range(DEPTH):
        lambda_init = 0.8 - 0.6 * math.exp(-0.3 * layer)
        xn = _rmsnorm(h, norm_w[layer])
        (da_q, da_k, da_v, da_z, ml_q, ml_k, ml_v, ml_o, ml_z, ml_i, ml_f,
         gate_a, gate_m) = [xn @ wp for wp in _split_columns(w_in[layer])]
        y_a = _diff_attention(da_q, da_k, da_v, lam[layer], subln_w[layer], rel_bias,
                              lambda_init) * jax.nn.silu(da_z)
        y_m = _mlstm(ml_q, ml_k, ml_v, ml_i, ml_f, ml_o, conv_w[layer], conv_b[layer],
                     b_if[layer], mh_w[layer]) * jax.nn.silu(ml_z)
        g_a = jax.nn.sigmoid(gate_a + b_gate[layer, 0])
        g_m = jax.nn.sigmoid(gate_m + b_gate[layer, 1])
        merged = g_a * (y_a @ w_pa[layer]) + g_m * (y_m @ w_pm[layer])
        h = h + merged @ w_out[layer]
    return _rmsnorm(h, norm_final)
```

```python
import math
from contextlib import ExitStack

import numpy as np
import concourse.bass as bass
import concourse.mybir as mybir
from concourse.bass_utils import run_bass_kernel_spmd

F32 = mybir.dt.float32
BF16 = mybir.dt.bfloat16
ALU = mybir.AluOpType
AF = mybir.ActivationFunctionType

D = 1024
NH = 8
D_IN = 11280
NORM_EPS = 1e-6
SUBLN_EPS = 1e-5
HEAD_LN_EPS = 1e-5
LAMBDA_INIT = 0.8 - 0.6 * math.exp(-0.3 * 0)
NCORES = 8
APPROX_RECIP = False
import os
CONV_ENG = tuple(os.environ.get('CONV_ENG', 'dve,dve').split(','))
S3_HP_ENG = os.environ.get('S3_HP_ENG', 'act')
S3_HN_ENG = os.environ.get('S3_HN_ENG', 'dve')
OWN_ENG = tuple(x for x in os.environ.get('OWN_ENG', 'act,dve,pool').split(',') if x)

C_DAQ, C_DAK, C_DAV, C_DAZ = 0, 1024, 2048, 3072
C_MLQ, C_MLK, C_MLV, C_MLO, C_MLZ = 4096, 5120, 6144, 7168, 8192
C_IF = 9216
C_GA, C_GM = 9232, 10256

K_NW, K_LAM, K_SUB, K_CH, K_CW, K_CB, K_MHW, K_BG = 0, 8, 264, 265, 273, 337, 353, 361
K_TOT = 384

ENGS = ["pe", "act", "dve", "pool", "sp"]
EPOCH_MAX = 30000


class Buf:
    __slots__ = ("w", "r", "x")

    def __init__(self, x=False):
        self.w = None
        self.r = []
        self.x = x


class Prog:
    def __init__(self, nc, es):
        self.nc = nc
        self.es = es
        self.q = {e: [] for e in ENGS}
        self.sem = {}
        self.cnt = {e: 0 for e in ENGS}
        self.epoch = {e: 0 for e in ENGS}
        self.waited = {e: {} for e in ENGS}
        self.nsem = 0
        for e in ENGS:
            self.sem[e] = self._newsem(f"s_{e}_0")
        self.dma_sems = {}

    def _newsem(self, name):
        self.nsem += 1
        return self.es.enter_context(self.nc.semaphore(name))

    def _next_token(self, eng):
        if self.cnt[eng] >= EPOCH_MAX:
            self.epoch[eng] += 1
            self.cnt[eng] = 0
            self.sem[eng] = self._newsem(f"s_{eng}_{self.epoch[eng]}")
        self.cnt[eng] += 1
        return (eng, self.epoch[eng], self.cnt[eng], self.sem[eng])

    def _need_wait(self, eng, tok):
        src, ep, val, _ = tok
        cur = self.waited[eng].get(src)
        if cur is not None and cur >= (ep, val):
            return False
        self.waited[eng][src] = (ep, val)
        return True

    def _collect(self, eng, reads, writes, extra=()):
        toks = list(extra)
        for b in reads:
            if b.w is not None:
                toks.append(b.w)
            if b.x:
                toks.extend(t for t in b.r if t[0] != eng)
        for b in writes:
            if b.w is not None:
                toks.append(b.w)
            toks.extend(b.r)
        best = {}
        for t in toks:
            k = t[0]
            if k not in best or (best[k][1], best[k][2]) < (t[1], t[2]):
                best[k] = t
        waits = []
        for k, t in best.items():
            if eng == "pe" and k == "pe":
                continue
            if self._need_wait(eng, t):
                waits.append((t[3], t[2]))
        return waits

    def _commit(self, tok, reads, writes):
        for b in writes:
            b.w = tok
            b.r = []
        for b in reads:
            b.r = [t for t in b.r if t[0] != tok[0]]
            b.r.append(tok)

    def op(self, eng, fn, reads=(), writes=()):
        waits = self._collect(eng, reads, writes)
        tok = self._next_token(eng)
        self.q[eng].append((waits, fn, (tok[3], 1)))
        self._commit(tok, reads, writes)
        return tok

    def group(self, eng, fns, reads=(), writes=()):
        waits = self._collect(eng, reads, writes)
        tok = self._next_token(eng)
        n = len(fns)
        for i, fn in enumerate(fns):
            self.q[eng].append((waits if i == 0 else [], fn, (tok[3], 1) if i == n - 1 else None))
        self._commit(tok, reads, writes)
        return tok

    def dma(self, eng, fn, key, reads=(), writes=()):
        waits = self._collect(eng, reads, writes)
        if key not in self.dma_sems:
            self.dma_sems[key] = [self._newsem(f"d_{len(self.dma_sems)}"), 0]
        ent = self.dma_sems[key]
        ent[1] += 16
        tok = (("dma", key), 0, ent[1], ent[0])
        self.q[eng].append((waits, fn, (ent[0], 16)))
        self._commit(tok, reads, writes)
        return tok

    def barrier(self):
        toks = []
        for e in ENGS:
            if self.cnt[e] > 0 or self.epoch[e] > 0:
                toks.append((e, self.epoch[e], self.cnt[e], self.sem[e]))
        for key, ent in self.dma_sems.items():
            toks.append((("dma", key), 0, ent[1], ent[0]))
        for e in ENGS:
            if not self.q[e]:
                continue
            waits = []
            for t in toks:
                if t[2] == 0 or (t[0] == e and (e not in OWN_ENG)):
                    continue
                if self._need_wait(e, t):
                    waits.append((t[3], t[2]))
            if waits:
                self.q[e].append((waits, None, None))

    def wait_all(self, eng, bufs):
        waits = self._collect(eng, [], bufs)
        self.q[eng].append((waits, None, None))

    def emit(self):
        nc = self.nc
        q = self.q

        def run(engine, lst):
            for waits, fn, sig in lst:
                for sem, val in waits:
                    engine.wait_ge(sem, val)
                if fn is None:
                    continue
                ins = fn(engine)
                if sig is not None:
                    ins.then_inc(sig[0], sig[1])

        with nc.Block() as block:
            if q["pe"]:
                @block.tensor
                def _(e):
                    run(e, q["pe"])
            if q["act"]:
                @block.scalar
                def _(e):
                    run(e, q["act"])
            if q["dve"]:
                @block.vector
                def _(e):
                    run(e, q["dve"])
            if q["pool"]:
                @block.gpsimd
                def _(e):
                    run(e, q["pool"])
            if q["sp"]:
                @block.sync
                def _(e):
                    run(e, q["sp"])


def build(NSEQ, S, dbg=False, stage=99):
    NT = S // 128
    NQ = S // 256
    N5 = S // 512
    NCH = S // 64
    nc = bass.Bass("TRN2", target_bir_lowering=False)
    x_d = nc.dram_tensor("x", [NSEQ, S, D], F32, kind="ExternalInput").ap()
    win_d = nc.dram_tensor("w_in", [D, D_IN], F32, kind="ExternalInput").ap()
    wpa_d = nc.dram_tensor("w_pa", [D, D], F32, kind="ExternalInput").ap()
    wpm_d = nc.dram_tensor("w_pm", [D, D], F32, kind="ExternalInput").ap()
    wout_d = nc.dram_tensor("w_out", [D, D], F32, kind="ExternalInput").ap()
    cst_d = nc.dram_tensor("cst", [128, K_TOT], F32, kind="ExternalInput").ap()
    bif_d = nc.dram_tensor("bif", [128, NT * 16], F32, kind="ExternalInput").ap()
    nfb_d = nc.dram_tensor("nfb", [128, D], F32, kind="ExternalInput").ap()
    biasg_d = nc.dram_tensor("biasg", [NH, 128, 768], F32, kind="ExternalInput").ap()
    maskc_d = nc.dram_tensor("maskc", [128, 768], F32, kind="ExternalInput").ap()
    out_d = nc.dram_tensor("out", [NSEQ, S, D], F32, kind="ExternalOutput").ap()
    if dbg:
        dya_d = nc.dram_tensor("dbg_ya", [128, NH, S], F32, kind="ExternalOutput").ap()
        dym_d = nc.dram_tensor("dbg_ym", [128, NH, S], F32, kind="ExternalOutput").ap()

    with ExitStack() as es:
        P = Prog(nc, es)

        def sb(name, shape, dt):
            return es.enter_context(nc.sbuf_tensor("sb_" + name, shape, dt))

        def ps(name, shape, dt):
            return es.enter_context(nc.psum_tensor("ps_" + name, shape, dt))

        xnT = sb("xnT", [128, 8, S], BF16); b_xnT = Buf()
        yaT = sb("yaT", [128, NH, S], BF16); b_ya = [Buf() for _ in range(NH)]
        ymT = sb("ymT", [128, NH, S], BF16); b_ym = [Buf() for _ in range(NH)]
        NWS = 6
        wsl = [sb(f"wsl{i}", [128, 8, 128], BF16) for i in range(NWS)]
        b_wsl = [Buf() for _ in range(NWS)]
        cst = sb("cst", [128, K_TOT], F32); b_cst = Buf()
        bif = sb("bif", [128, NT * 16], F32)
        nfb = sb("nfb", [128, D], F32)
        maskc = sb("maskc", [128, 768], F32)
        identf = sb("identf", [128, 128], F32)
        ident = sb("ident", [128, 128], BF16)
        ones_b = sb("ones_b", [128, 128], BF16)
        ones_f = sb("ones_f", [128, 128], F32)
        onesA = sb("onesA", [128, 128], F32)
        onesB = sb("onesB", [128, 128], F32)
        tri = sb("tri", [128, 128], F32)
        patA = sb("patA", [128, 128], BF16)
        patB = sb("patB", [128, 128], BF16)
        der = sb("der", [128, 16], F32)
        NXT = 3
        xt = [sb(f"xt{i}", [128, D], F32) for i in range(NXT)]; b_xt = [Buf() for _ in range(NXT)]
        xs = [sb(f"xs{i}", [128, D], BF16) for i in range(NXT)]; b_xs = [Buf() for _ in range(NXT)]
        sm = sb("sm", [128, 16], F32); b_sm = Buf()
        G = sb("G", [128, NT, 16], F32); b_G = Buf()
        LF = sb("LF", [128, NT, 8], F32)
        BC = sb("BC", [128, NT, 8], F32)
        SK = sb("SK", [128, NT, 8], F32)
        ENB = sb("ENB", [128, NT, 8], F32)
        EB = sb("EB", [128, NCH, 8], F32)
        SCB = 24576
        scrB = sb("scrB", [128, SCB], BF16)
        SCF = 3584
        scrF = sb("scrF", [128, SCF], F32)
        kT = scrB[:, 2048:2048 + S]; zs = scrB[:, 4096:4096 + S]
        QQ = scrB[:, 11520:11520 + NQ * 512].rearrange("p (j c) -> p j c", c=512)
        vt = scrB[:, 6144:6144 + NT * 128].rearrange("p (t d) -> p t d", d=128)
        BM = scrB[:, 8192:8192 + 1536].rearrange("p (o c) -> p o c", c=512)
        PT = [scrB[:, 9728 + i * 512: 9728 + (i + 1) * 512] for i in range(3)] + [scrB[:, 17920:18432]]
        sq = [scrB[:, 11264:11264 + 256], scrB[:, 17664:17664 + 256]]
        vTf = scrB[:, 15616:15616 + S]
        bgf = scrF[:, 0:768]
        T_ = scrF[:, 768:1280]; rinv = scrF[:, 1280:1792]
        Od = [scrF[:, 1792:2048], scrF[:, 2304:2560]]; lnv = [scrF[:, 2048:2304], scrF[:, 2816:3072]]; yy = [scrF[:, 2560:2816], scrF[:, 3072:3328]]
        mqT = scrB[:, 0:S]; mkT = scrB[:, 2048:2048 + S]
        mv1 = scrB[:, 4096:4096 + NT * 130].rearrange("p (t d) -> p t d", d=130)
        mvTf = scrB[:, 6176:6176 + S]
        OGZ = scrB[:, 8224:8224 + S]; ZG = scrB[:, 10272:10272 + S]
        scTa = scrB[:, 12320:12320 + NT * 128].rearrange("p (t d) -> p t d", d=128)
        kpA = scrB[:, 14368:14368 + NT * 128].rearrange("p (t d) -> p t d", d=128)
        kpB = scrB[:, 16416:16416 + NT * 128].rearrange("p (t d) -> p t d", d=128)
        Cbfa = scrB[:, 18480:18480 + (NCH + 1) * 130].rearrange("p (c d) -> p c d", d=130)
        diagw = scrB[:, 16416:16416 + 1024].rearrange("p (k d) -> p k d", d=128)
        Ubf = [scrB[:, 17440 + i * 516: 17440 + i * 516 + 515] for i in range(2)]
        hn4 = [scrB[:, 22784 + i * 512: 22784 + (i + 1) * 512] for i in range(2)]
        UQt = [scrF[:, i * 515: (i + 1) * 515] for i in range(2)]
        acct = [scrF[:, 1032 + i * 512: 1032 + (i + 1) * 512] for i in range(2)]
        Ub = [scrF[:, 2056 + i * 130: 2056 + i * 130 + 129] for i in range(2)]
        hp4 = [scrF[:, 2320 + i * 512: 2320 + (i + 1) * 512] for i in range(2)]
        ms4 = [scrF[:, 3344 + i * 64: 3344 + (i + 1) * 64] for i in range(2)]
        mgT = scrB[:, 0:8 * S].rearrange("p (c s) -> p c s", s=S)
        woutb = scrB[:, 16384:24576].rearrange("p (c n) -> p c n", n=1024)
        sga = scrF[:, 0:512]; sgm = scrF[:, 512:1024]; t1 = scrF[:, 1024:1536]; t2 = scrF[:, 1536:2048]
        bank = [ps(f"bank{i}", [128, 512], F32) for i in range(8)]
        b_bank = [Buf(x=True) for _ in range(8)]

        class NS:
            pass
        B = NS()

        def fresh(*names):
            for n in names:
                setattr(B, n, Buf())

        b_const = Buf()
        P.dma("sp", lambda e: e.dma_start(out=cst[:], in_=cst_d[:, :]), "c0", writes=[b_cst])
        P.dma("sp", lambda e: e.dma_start(out=bif[:], in_=bif_d[:, :]), "c0", writes=[b_cst])
        P.dma("sp", lambda e: e.dma_start(out=nfb[:], in_=nfb_d[:, :]), "c0", writes=[b_cst])
        P.dma("sp", lambda e: e.dma_start(out=maskc[:], in_=maskc_d[:, :]), "c0", writes=[b_cst])
        P.op("pool", lambda e: e.memset(identf[:], 1.0), writes=[b_const])
        P.op("pool", lambda e: e.affine_select(out=identf[:], in_=identf[:], pattern=[[-1, 128]],
                                                compare_op=ALU.is_equal, fill=0.0, base=0, channel_multiplier=1),
             reads=[b_const], writes=[b_const])
        P.op("pool", lambda e: e.memset(ones_f[:], 1.0), writes=[b_const])
        P.op("pool", lambda e: e.memset(onesA[:], 0.0), writes=[b_const])
        P.op("pool", lambda e: e.memset(onesB[:], 0.0), writes=[b_const])
        P.op("pool", lambda e: e.memset(onesA[0:64, :], 1.0), writes=[b_const])
        P.op("pool", lambda e: e.memset(onesB[64:128, :], 1.0), writes=[b_const])
        P.op("pool", lambda e: e.memset(tri[:], 1.0), writes=[b_const])
        P.op("pool", lambda e: e.affine_select(out=tri[:], in_=tri[:], pattern=[[1, 128]],
                                                compare_op=ALU.is_ge, fill=0.0, base=0, channel_multiplier=-1),
             reads=[b_const], writes=[b_const])
        P.op("pool", lambda e: e.memset(tri[0:64, 64:128], 0.0), reads=[b_const], writes=[b_const])
        P.op("dve", lambda e: e.tensor_copy(out=ident[:], in_=identf[:]), reads=[b_const], writes=[b_const])
        P.op("dve", lambda e: e.memset(ones_b[:], 1.0), writes=[b_const])
        P.op("dve", lambda e: e.memset(patA[:], 0.0), writes=[b_const])
        P.op("dve", lambda e: e.memset(patB[:], 0.0), writes=[b_const])
        P.op("dve", lambda e: e.memset(patA[:, 0:64], 1.0), writes=[b_const])
        P.op("dve", lambda e: e.memset(patB[:, 64:128], 1.0), writes=[b_const])
        P.op("dve", lambda e: e.tensor_tensor(out=cst[:, K_LAM:K_LAM + 64], in0=cst[:, K_LAM:K_LAM + 64],
                                              in1=cst[:, K_LAM + 64:K_LAM + 128], op=ALU.mult), reads=[b_cst], writes=[b_cst])
        P.op("dve", lambda e: e.tensor_tensor(out=cst[:, K_LAM + 128:K_LAM + 192], in0=cst[:, K_LAM + 128:K_LAM + 192],
                                              in1=cst[:, K_LAM + 192:K_LAM + 256], op=ALU.mult), reads=[b_cst], writes=[b_cst])
        P.op("dve", lambda e: e.tensor_reduce(out=der[:, 2:3], in_=cst[:, K_LAM:K_LAM + 64], axis=mybir.AxisListType.X, op=ALU.add),
             reads=[b_cst], writes=[b_const])
        P.op("dve", lambda e: e.tensor_reduce(out=der[:, 3:4], in_=cst[:, K_LAM + 128:K_LAM + 192], axis=mybir.AxisListType.X, op=ALU.add),
             reads=[b_cst], writes=[b_const])
        P.op("act", lambda e: e.activation(out=der[:, 2:4], in_=der[:, 2:4], func=AF.Exp), reads=[b_const], writes=[b_const])
        P.op("dve", lambda e: e.scalar_tensor_tensor(out=der[:, 0:1], in0=der[:, 3:4], scalar=-LAMBDA_INIT, in1=der[:, 2:3],
                                                     op0=ALU.add, op1=ALU.subtract), reads=[b_const], writes=[b_const])
        P.op("dve", lambda e: e.tensor_scalar(out=der[:, 1:2], in0=cst[:, K_SUB:K_SUB + 1], scalar1=(1.0 - LAMBDA_INIT), scalar2=None,
                                              op0=ALU.mult), reads=[b_cst, b_const], writes=[b_const])

        P.op("dve", lambda e: e.tensor_scalar(out=der[:, 8:16], in0=cst[:, K_CH:K_CH + 8], scalar1=-1.0, scalar2=None, op0=ALU.mult),
             reads=[b_cst, b_const], writes=[b_const])
        state = {"w": 0, "pb": 0}

        def load_wblock(w_ap, col0, ncols=128):
            i = state["w"] % NWS
            state["w"] += 1
            src = w_ap[:, col0:col0 + ncols].rearrange("(c p) n -> p c n", p=128)
            P.dma("pool", lambda e: e.dma_start(out=wsl[i][:, :, 0:ncols], in_=src), ("w", i), writes=[b_wsl[i]])
            return wsl[i], b_wsl[i]

        pref = {}

        def prefetch(w_ap, col0, ncols=128):
            pref[(id(w_ap), col0)] = load_wblock(w_ap, col0, ncols)

        def get_w(w_ap, col0, ncols=128):
            k = (id(w_ap), col0)
            if k in pref:
                return pref.pop(k)
            return load_wblock(w_ap, col0, ncols)

        def next_pb():
            k = state["pb"] % 2
            state["pb"] += 1
            return k

        def proj_fm(wt, wb, rhsT, rhs_bufs, evac):
            for tt in range(N5):
                bk = next_pb()
                fns = [(lambda e, c=c, bk=bk, tt=tt: e.matmul(bank[bk][:, 0:512], lhsT=wt[:, c, :],
                                                             rhs=rhsT[:, c, tt * 512:(tt + 1) * 512],
                                                             start=(c == 0), stop=(c == 7))) for c in range(8)]
                P.group("pe", fns, reads=[wb] + rhs_bufs, writes=[b_bank[bk]])
                evac(tt, bank[bk], b_bank[bk])

        def proj_tm(wt, wb, evac):
            for g in range(NT // 4):
                bk = next_pb()
                for a in range(4):
                    t = g * 4 + a
                    fns = [(lambda e, c=c, bk=bk, a=a, t=t: e.matmul(bank[bk][:, a * 128:(a + 1) * 128],
                                                                   lhsT=xnT[:, c, t * 128:(t + 1) * 128], rhs=wt[:, c, :],
                                                                   start=(c == 0), stop=(c == 7))) for c in range(8)]
                    P.group("pe", fns, reads=[wb, b_xnT], writes=[b_bank[bk]])
                evac(g, bank[bk], b_bank[bk])

        def proj_v1(wt, wb, vT_ap, vT_buf):
            proj_fm(wt, wb, xnT, [b_xnT], lambda tt, bkap, bb: P.op(
                "dve", lambda e: e.tensor_copy(out=vT_ap[:, tt * 512:(tt + 1) * 512], in_=bkap[:, 0:512]), reads=[bb], writes=[vT_buf]))

        def proj_v2(vT_ap, vT_buf, evac):
            for g in range(NT // 4):
                bk = next_pb()
                fns = [(lambda e, a=a, bk=bk, g=g: e.matmul(bank[bk][:, a * 128:(a + 1) * 128], lhsT=vT_ap[:, (g * 4 + a) * 128:(g * 4 + a + 1) * 128],
                                                          rhs=ident[:], start=True, stop=True)) for a in range(4)]
                P.group("pe", fns, reads=[vT_buf, b_const], writes=[b_bank[bk]])
                evac(g, bank[bk], b_bank[bk])

        def rsqrt_act(dst, src, scale, eps, rbufs, wbufs):
            P.op("act", lambda e: e.activation(out=dst, in_=src, func=AF.Ln, bias=eps, scale=scale), reads=rbufs, writes=wbufs)
            P.op("act", lambda e: e.activation(out=dst, in_=dst, func=AF.Exp, scale=-0.5), reads=wbufs, writes=wbufs)

        b_out = Buf()
        b_bgf = Buf()

        for b in range(NSEQ):
            if stage < 2:
                break
            for t in range(NT):
                i = t % NXT
                P.dma("sp", lambda e, i=i, t=t, b=b: e.dma_start(out=xt[i][:], in_=x_d[b, t * 128:(t + 1) * 128, :]), ("x", i), writes=[b_xt[i]])
                P.op("act", lambda e, i=i: e.activation(out=xs[i][:], in_=xt[i][:], func=AF.Square, accum_out=sm[:, 0:1]),
                     reads=[b_xt[i]], writes=[b_xs[i], b_sm])
                rsqrt_act(sm[:, 1:2], sm[:, 0:1], 1.0 / D, NORM_EPS, [b_sm], [b_sm])
                P.op("act", lambda e, i=i: e.activation(out=xs[i][:], in_=xt[i][:], func=AF.Copy, scale=sm[:, 1:2]),
                     reads=[b_xt[i], b_sm], writes=[b_xs[i]])
                for half in range(2):
                    bk = next_pb()
                    fns = [(lambda e, a=a, bk=bk, i=i, half=half: e.matmul(bank[bk][:, a * 128:(a + 1) * 128],
                                                                          lhsT=xs[i][:, (half * 4 + a) * 128:(half * 4 + a + 1) * 128],
                                                                          rhs=ident[:], start=True, stop=True)) for a in range(4)]
                    P.group("pe", fns, reads=[b_xs[i], b_const], writes=[b_bank[bk]])
                    for a in range(4):
                        c = half * 4 + a
                        P.op("dve", lambda e, a=a, c=c, bk=bk, t=t: e.tensor_scalar(out=xnT[:, c, t * 128:(t + 1) * 128],
                                                                                 in0=bank[bk][:, a * 128:(a + 1) * 128],
                                                                                 scalar1=cst[:, K_NW + c:K_NW + c + 1], scalar2=None, op0=ALU.mult),
                             reads=[b_bank[bk], b_cst], writes=[b_xnT])

            if stage < 3:
                break
            wt, wb = load_wblock(win_d, C_IF, 16)
            bk = next_pb()
            for t in range(NT):
                fns = [(lambda e, c=c, t=t, bk=bk, wt=wt: e.matmul(bank[bk][:, t * 16:(t + 1) * 16], lhsT=xnT[:, c, t * 128:(t + 1) * 128],
                                                                 rhs=wt[:, c, 0:16], start=(c == 0), stop=(c == 7))) for c in range(8)]
                P.group("pe", fns, reads=[wb, b_xnT], writes=[b_bank[bk]])
            Gf = G[:].rearrange("p t g -> p (t g)")
            P.op("dve", lambda e, bk=bk: e.tensor_tensor(out=Gf, in0=bank[bk][:, 0:NT * 16], in1=bif[:], op=ALU.add),
                 reads=[b_bank[bk], b_cst], writes=[b_G])
            P.op("act", lambda e: e.activation(out=LF[:], in_=G[:, :, 8:16], func=AF.Exp, scale=-1.0), reads=[b_G], writes=[b_G])
            P.op("act", lambda e: e.activation(out=LF[:], in_=LF[:], func=AF.Ln, bias=1.0, scale=1.0), reads=[b_G], writes=[b_G])
            P.op("dve", lambda e: e.tensor_scalar(out=LF[:], in0=LF[:], scalar1=-1.0, scalar2=None, op0=ALU.mult), reads=[b_G], writes=[b_G])
            bk = next_pb()
            for t in range(NT):
                P.op("pe", lambda e, t=t, bk=bk: e.matmul(bank[bk][:, t * 8:(t + 1) * 8], lhsT=tri[:], rhs=LF[:, t, :], start=True, stop=True),
                     reads=[b_G, b_const], writes=[b_bank[bk]])
            P.op("dve", lambda e, bk=bk: e.tensor_copy(out=BC[:].rearrange("p t g -> p (t g)"), in_=bank[bk][:, 0:NT * 8]),
                 reads=[b_bank[bk]], writes=[b_G])
            bk = next_pb()
            for t in range(NT):
                for hf in range(2):
                    c = 2 * t + hf
                    P.op("pe", lambda e, t=t, hf=hf, c=c, bk=bk: e.matmul(bank[bk][:, c * 8:(c + 1) * 8], lhsT=(onesA if hf == 0 else onesB)[:],
                                                                         rhs=LF[:, t, :], start=True, stop=True),
                         reads=[b_G, b_const], writes=[b_bank[bk]])
            P.op("act", lambda e, bk=bk: e.activation(out=EB[:].rearrange("p c g -> p (c g)"), in_=bank[bk][:, 0:NCH * 8], func=AF.Exp),
                 reads=[b_bank[bk]], writes=[b_G])
            P.op("dve", lambda e: e.tensor_tensor(out=SK[:], in0=G[:, :, 0:8], in1=BC[:], op=ALU.subtract), reads=[b_G], writes=[b_G])
            P.op("dve", lambda e: e.tensor_scalar(out=SK[:], in0=SK[:], scalar1=-0.5 * math.log(128.0), scalar2=None, op0=ALU.add),
                 reads=[b_G], writes=[b_G])
            P.op("act", lambda e: e.activation(out=SK[:], in_=SK[:], func=AF.Exp), reads=[b_G], writes=[b_G])
            P.op("act", lambda e: e.activation(out=ENB[:], in_=BC[:], func=AF.Exp, scale=-1.0), reads=[b_G], writes=[b_G])

            if stage < 4:
                break
            for h in range(NH):
                for cc in (C_DAV, C_DAQ, C_DAK, C_DAZ):
                    prefetch(win_d, cc + h * 128)
                P.barrier()
                fresh("qT", "kT", "zs", "vt", "vTf", "BM", "T", "rinv")
                b_Od = [Buf() for _ in range(2)]; b_sq = [Buf() for _ in range(2)]; b_lnv = [Buf() for _ in range(2)]; b_yy = [Buf() for _ in range(2)]
                b_PT = [Buf() for _ in range(4)]
                P.op("dve", lambda e: e.memset(QQ[64:128, :, 0:256], 0.0), writes=[B.qT])
                P.op("dve", lambda e: e.memset(QQ[0:64, :, 256:512], 0.0), writes=[B.qT])
                if h == 0:
                    P.dma("sp", lambda e, h=h: e.dma_start(out=bgf, in_=biasg_d[h, :, :]), "bg", writes=[b_bgf])
                for o in range(3):
                    for dup in range(2):
                        P.op("dve", lambda e, o=o, dup=dup, h=h: e.scalar_tensor_tensor(out=BM[:, o, dup * 256:(dup + 1) * 256], in0=bgf[:, o * 256:(o + 1) * 256],
                                                                                      scalar=der[:, 8 + h:9 + h], in1=maskc[:, o * 256:(o + 1) * 256],
                                                                                      op0=ALU.add, op1=ALU.add),
                             reads=[b_bgf, b_cst, b_const], writes=[B.BM])
                wt, wb = get_w(win_d, C_DAV + h * 128)
                proj_v1(wt, wb, vTf, B.vTf)
                wt, wb = get_w(win_d, C_DAQ + h * 128)
                if h + 1 < NH:
                    P.dma("sp", lambda e, h=h: e.dma_start(out=bgf, in_=biasg_d[h + 1, :, :]), "bg", reads=[], writes=[b_bgf])

                def q_evac(tt, bkap, bb):
                    for part in range(2):
                        pr = slice(part * 64, (part + 1) * 64)
                        P.op("act", lambda e, pr=pr, part=part: e.activation(
                            out=QQ[pr, 2 * tt:2 * tt + 2, part * 256:(part + 1) * 256],
                            in_=bkap[pr, 0:512].rearrange("p (a c) -> p a c", a=2), func=AF.Copy, scale=0.125),
                            reads=[bb], writes=[B.qT])
                proj_fm(wt, wb, xnT, [b_xnT], q_evac)
                wt, wb = get_w(win_d, C_DAK + h * 128)
                proj_fm(wt, wb, xnT, [b_xnT], lambda tt, bkap, bb: P.op(
                    "dve", lambda e: e.tensor_copy(out=kT[:, tt * 512:(tt + 1) * 512], in_=bkap[:, 0:512]),
                    reads=[bb], writes=[B.kT]))
                wt, wb = get_w(win_d, C_DAZ + h * 128)
                proj_fm(wt, wb, xnT, [b_xnT], lambda tt, bkap, bb: P.op(
                    "act", lambda e: e.activation(out=zs[:, tt * 512:(tt + 1) * 512], in_=bkap[:, 0:512], func=AF.Silu),
                    reads=[bb], writes=[B.zs]))
                proj_v2(vTf, B.vTf, lambda g, bkap, bb: P.op(
                    "dve", lambda e: e.tensor_copy(out=vt[:, g * 4:(g + 1) * 4, :], in_=bkap[:, 0:512].rearrange("p (a d) -> p a d", a=4)),
                    reads=[bb], writes=[B.vt]))

                pairs = [(j, kt) for j in range(NQ) for kt in range(2 * j + 2)]

                def emit_S(i):
                    j, kt = pairs[i]
                    sbk = (2, 3, 0)[i % 3]
                    near = kt >= 2 * j - 1
                    fns = []
                    rd = [B.kT, B.qT]
                    if near:
                        o = kt - 2 * j + 1
                        fns.append(lambda e: e.matmul(bank[sbk][:, 0:512], lhsT=ident[:], rhs=BM[:, o, :], start=True, stop=False))
                        rd = rd + [B.BM, b_const]
                    fns.append(lambda e: e.matmul(bank[sbk][:, 0:512], lhsT=kT[:, kt * 128:(kt + 1) * 128], rhs=QQ[:, j, :],
                                                  start=(not near), stop=True))
                    P.group("pe", fns, reads=rd, writes=[b_bank[sbk]])

                pending = []

                def epilogue2(pj, ph):
                    qs = slice(pj * 256, (pj + 1) * 256)
                    k2 = pj % 2
                    sbk2 = 1
                    P.op("pe", lambda e, sbk2=sbk2: e.matmul(bank[sbk2][:, 0:256], lhsT=ones_b[:], rhs=sq[k2], start=True, stop=True),
                         reads=[b_sq[k2], b_const], writes=[b_bank[sbk2]])
                    rsqrt_act(lnv[k2], bank[sbk2][:, 0:256], 1.0 / 128.0, SUBLN_EPS, [b_bank[sbk2]], [b_lnv[k2]])
                    P.op("dve", lambda e: e.scalar_tensor_tensor(out=yy[k2], in0=Od[k2], scalar=der[:, 1:2], in1=lnv[k2], op0=ALU.mult, op1=ALU.mult),
                         reads=[b_Od[k2], b_lnv[k2], b_const], writes=[b_yy[k2]])
                    P.op("dve", lambda e, qs=qs, ph=ph: e.tensor_tensor(out=yaT[:, ph, qs], in0=yy[k2], in1=zs[:, qs], op=ALU.mult),
                         reads=[b_yy[k2], B.zs], writes=[b_ya[ph]])

                emit_S(0)
                for i, (j, kt) in enumerate(pairs):
                    if i + 1 < len(pairs):
                        emit_S(i + 1)
                    sbk = (2, 3, 0)[i % 3]
                    pt = i % 4
                    near = kt >= 2 * j - 1
                    P.op("act", lambda e, sbk=sbk, pt=pt: e.activation(out=PT[pt], in_=bank[sbk][:, 0:512], func=AF.Exp),
                         reads=[b_bank[sbk]], writes=[b_PT[pt]])
                    first = (kt == 0)
                    last = (kt == 2 * j + 1)
                    ob = 4 + 2 * (j % 2)
                    rb = ob + 1
                    P.op("pe", lambda e, kt=kt, pt=pt, first=first, last=last, ob=ob: e.matmul(bank[ob][:, 0:512], lhsT=vt[:, kt, :], rhs=PT[pt],
                                                                                              start=first, stop=last),
                         reads=[B.vt, b_PT[pt]], writes=[b_bank[ob]])
                    P.op("pe", lambda e, pt=pt, first=first, last=last, rb=rb: e.matmul(bank[rb][:, 0:512], lhsT=ones_b[:], rhs=PT[pt],
                                                                                       start=first, stop=last),
                         reads=[b_const, b_PT[pt]], writes=[b_bank[rb]])
                    if last:
                        qs = slice(j * 256, (j + 1) * 256)
                        if APPROX_RECIP:
                            P.op("dve", lambda e, rb=rb: e.reciprocal_approx_fast(out=rinv, in_=bank[rb][:, 0:512]), reads=[b_bank[rb]], writes=[B.rinv])
                        else:
                            P.op("dve", lambda e, rb=rb: e.reciprocal(out=rinv, in_=bank[rb][:, 0:512]), reads=[b_bank[rb]], writes=[B.rinv])
                        P.op("dve", lambda e, ob=ob: e.tensor_tensor(out=T_, in0=bank[ob][:, 0:512], in1=rinv, op=ALU.mult),
                             reads=[b_bank[ob], B.rinv], writes=[B.T])
                        k1 = j % 2
                        P.op("dve", lambda e, k1=k1: e.scalar_tensor_tensor(out=Od[k1], in0=T_[:, 256:512], scalar=der[:, 0:1], in1=T_[:, 0:256],
                                                                            op0=ALU.mult, op1=ALU.add), reads=[B.T, b_const], writes=[b_Od[k1]])
                        P.op("dve", lambda e, k1=k1: e.tensor_tensor(out=sq[k1], in0=Od[k1], in1=Od[k1], op=ALU.mult), reads=[b_Od[k1]], writes=[b_sq[k1]])
                        if pending:
                            pj, ph = pending.pop(0)
                            epilogue2(pj, ph)
                        pending.append((j, h))
                while pending:
                    pj, ph = pending.pop(0)
                    epilogue2(pj, ph)

                for cc in (C_MLV, C_MLQ, C_MLK):
                    prefetch(win_d, cc + h * 128)
                P.barrier()
                fresh("mq", "mk", "mv", "vTf", "ogz", "zg", "scT", "kp")
                b_UQ = [Buf() for _ in range(2)]; b_acc = [Buf() for _ in range(2)]; b_U = [Buf() for _ in range(2)]
                b_Cbf = [Buf() for _ in range(NCH // 4 + 1)]
                b_hp = [[Buf() for _ in range(4)] for _ in range(2)]; b_hn = [Buf() for _ in range(2)]; b_ms = [Buf() for _ in range(2)]
                b_st = [[Buf() for _ in range(4)] for _ in range(2)]; b_mv = [Buf() for _ in range(2)]; b_rs = [Buf() for _ in range(2)]
                P.op("dve", lambda e: e.memset(mv1[:, :, 128:130], 1.0), writes=[B.mv])
                P.op("dve", lambda e: e.memset(Cbfa[:, 0, :], 0.0), writes=[b_Cbf[0]])
                wt, wb = get_w(win_d, C_MLV + h * 128)
                proj_v1(wt, wb, mvTf, B.vTf)
                b_dg = Buf()
                for qk in range(2):
                    for j in range(4):
                        cwj = cst[:, K_CW + (qk * 4 + j) * 8 + h:K_CW + (qk * 4 + j) * 8 + h + 1]
                        P.op("dve", lambda e, qk=qk, j=j, cwj=cwj: e.tensor_scalar(out=diagw[:, qk * 4 + j, :], in0=identf[:], scalar1=cwj, scalar2=None, op0=ALU.mult),
                             reads=[b_const, b_cst], writes=[b_dg])
                uctr = 0
                for qk, col0, dstT in ((0, C_MLQ, mqT), (1, C_MLK, mkT)):
                    wt, wb = get_w(win_d, col0 + h * 128)
                    cb = cst[:, K_CB + qk * 8 + h:K_CB + qk * 8 + h + 1]
                    dbuf = B.mq if qk == 0 else B.mk
                    P.op("dve", lambda e, u=uctr % 2: e.memset(Ubf[u][:, 0:3], 0.0), writes=[b_UQ[uctr % 2]])
                    pend = []

                    def conv_emit(tt, u, qk=qk, cb=cb, dstT=dstT, dbuf=dbuf):
                        cbk = 2 + (tt % 2)
                        fns = [(lambda e, j=j: e.matmul(bank[cbk][:, 0:512], lhsT=diagw[:, qk * 4 + j, :], rhs=Ubf[u][:, j:j + 512], start=(j == 0), stop=(j == 3)))
                               for j in range(4)]
                        P.group("pe", fns, reads=[b_dg, b_UQ[u]], writes=[b_bank[cbk]])
                        P.op("act", lambda e: e.activation(out=dstT[:, tt * 512:(tt + 1) * 512], in_=bank[cbk][:, 0:512], func=AF.Silu, bias=cb, scale=1.0),
                             reads=[b_bank[cbk], b_cst], writes=[dbuf])

                    def qk_evac(tt, bkap, bb, pend=pend, conv_emit=conv_emit):
                        nonlocal uctr
                        u = uctr % 2
                        uctr += 1
                        P.op("act", lambda e: e.activation(out=Ubf[u][:, 3:515], in_=bkap[:, 0:512], func=AF.Copy), reads=[bb], writes=[b_UQ[u]])
                        if pend:
                            conv_emit(*pend.pop())
                        if tt < N5 - 1:
                            P.op("dve", lambda e: e.tensor_copy(out=Ubf[1 - u][:, 0:3], in_=Ubf[u][:, 512:515]), reads=[b_UQ[u]], writes=[b_UQ[1 - u]])
                        pend.append((tt, u))
                    proj_fm(wt, wb, xnT, [b_xnT], qk_evac)
                    conv_emit(*pend.pop())
                proj_v2(mvTf, B.vTf, lambda g, bkap, bb: P.op(
                    "dve", lambda e: e.tensor_copy(out=mv1[:, g * 4:(g + 1) * 4, 0:128], in_=bkap[:, 0:512].rearrange("p (a d) -> p a d", a=4)),
                    reads=[bb], writes=[B.mv]))
                for t in range(NT):
                    ts_ = slice(t * 128, (t + 1) * 128)
                    sbk = 2 + (t % 2)
                    kbk = 4 + (t % 2)
                    P.op("pe", lambda e, ts_=ts_, sbk=sbk: e.matmul(bank[sbk][:, 0:128], lhsT=mkT[:, ts_], rhs=mqT[:, ts_], start=True, stop=True),
                         reads=[B.mk, B.mq], writes=[b_bank[sbk]])
                    P.op("pe", lambda e, ts_=ts_, kbk=kbk: e.matmul(bank[kbk][:, 0:128], lhsT=mkT[:, ts_], rhs=ident[:], start=True, stop=True),
                         reads=[B.mk, b_const], writes=[b_bank[kbk]])
                    P.op("dve", lambda e, sbk=sbk, t=t, h=h: e.scalar_tensor_tensor(out=scTa[:, t, :], in0=bank[sbk][:, 0:128], scalar=SK[:, t, h:h + 1], in1=tri[:],
                                                                                  op0=ALU.mult, op1=ALU.mult),
                         reads=[b_bank[sbk], b_G, b_const], writes=[B.scT])
                    P.op("act", lambda e, kbk=kbk, t=t, h=h: e.activation(out=kpA[:, t, :], in_=bank[kbk][:, 0:128], func=AF.Copy, scale=SK[:, t, h:h + 1]),
                         reads=[b_bank[kbk], b_G], writes=[B.kp])
                zo_jobs = []
                wz_t, wz_b = get_w(win_d, C_MLZ + h * 128)
                wo_t, wo_b = get_w(win_d, C_MLO + h * 128)

                def zo_group(kind, tt):
                    wt_, wb_ = (wz_t, wz_b) if kind == "z" else (wo_t, wo_b)
                    bk = next_pb()
                    fns = [(lambda e, c=c: e.matmul(bank[bk][:, 0:512], lhsT=wt_[:, c, :], rhs=xnT[:, c, tt * 512:(tt + 1) * 512], start=(c == 0), stop=(c == 7)))
                           for c in range(8)]
                    P.group("pe", fns, reads=[wb_, b_xnT], writes=[b_bank[bk]])
                    sl = slice(tt * 512, (tt + 1) * 512)
                    if kind == "z":
                        P.op("act", lambda e: e.activation(out=ZG[:, sl], in_=bank[bk][:, 0:512], func=AF.Silu), reads=[b_bank[bk]], writes=[B.zg])
                    else:
                        P.op("act", lambda e: e.activation(out=OGZ[:, sl], in_=bank[bk][:, 0:512], func=AF.Sigmoid), reads=[b_bank[bk]], writes=[B.ogz])
                        P.op("dve", lambda e: e.tensor_tensor(out=OGZ[:, sl], in0=OGZ[:, sl], in1=ZG[:, sl], op=ALU.mult), reads=[B.ogz, B.zg], writes=[B.ogz])
                for kind in ("z", "o"):
                    for tt in range(N5):
                        zo_jobs.append((kind, tt))
                dbanks = (6, 7, 4, 5)
                for c in range(NCH):
                    t, hf = c // 2, c % 2
                    db = dbanks[c % 4]
                    ps_ = slice(hf * 64, (hf + 1) * 64)
                    P.op("pe", lambda e, db=db, ps_=ps_, t=t: e.matmul(bank[db][:, 0:129], lhsT=kpA[ps_, t, :], rhs=mv1[ps_, t, 0:129], start=True, stop=True),
                         reads=[B.kp, B.mv], writes=[b_bank[db]])
                    u = c % 2
                    if c == 0:
                        P.op("dve", lambda e, db=db: e.tensor_copy(out=Ub[0], in_=bank[db][:, 0:129]), reads=[b_bank[db]], writes=[b_U[0]])
                    else:
                        P.op("dve", lambda e, db=db, u=u, c=c, h=h: e.scalar_tensor_tensor(out=Ub[u], in0=Ub[1 - u], scalar=EB[:, c - 1, h:h + 1], in1=bank[db][:, 0:129],
                                                                                        op0=ALU.mult, op1=ALU.add),
                             reads=[b_bank[db], b_G, b_U[1 - u]], writes=[b_U[u]])
                    if c < NCH - 1:
                        P.op("act", lambda e, u=u, c=c, h=h: e.activation(out=Cbfa[:, c + 1, 0:129], in_=Ub[u], func=AF.Copy, scale=EB[:, c, h:h + 1]),
                             reads=[b_U[u], b_G], writes=[b_Cbf[(c + 1) // 4]])
                    if c % 4 == 1 and zo_jobs:
                        zo_group(*zo_jobs.pop(0))
                while zo_jobs:
                    zo_group(*zo_jobs.pop(0))
                def s3_xy(g):
                    xb = 6 + (g % 2)
                    yb = 2 + (g % 2)
                    m_ = ms4[g % 2]
                    for a4 in range(4):
                        t = g * 4 + a4
                        c0 = 2 * t
                        xs_ = slice(a4 * 128, (a4 + 1) * 128)
                        for (osl, rs) in ((xs_, slice(0, 128)), (slice(t, t + 1), slice(128, 129))):
                            ob_ = xb if rs.start == 0 else yb
                            fns = [lambda e, c0=c0, ob_=ob_, osl=osl, rs=rs: e.matmul(bank[ob_][0:64, osl], lhsT=mqT[:, c0 * 64:(c0 + 1) * 64], rhs=Cbfa[:, c0, rs],
                                                                                    start=True, stop=False, tile_position=(0, 0)),
                                   lambda e, c0=c0, ob_=ob_, osl=osl, rs=rs: e.matmul(bank[ob_][64:128, osl], lhsT=mqT[:, (c0 + 1) * 64:(c0 + 2) * 64], rhs=Cbfa[:, c0 + 1, rs],
                                                                                    start=True, stop=False, tile_position=(0, 64)),
                                   lambda e, t=t, ob_=ob_, osl=osl, rs=rs: e.matmul(bank[ob_][:, osl], lhsT=scTa[:, t, :], rhs=mv1[:, t, rs], start=False, stop=True)]
                            rd = [B.scT, B.mv, B.mq, b_Cbf[c0 // 4], b_Cbf[(c0 + 1) // 4]]
                            P.group("pe", fns, reads=rd, writes=[b_bank[ob_]])
                def s3_pa(g):
                    xb = 6 + (g % 2)
                    yb = 2 + (g % 2)
                    m_ = ms4[g % 2]
                    g4 = slice(g * 4, (g + 1) * 4)
                    bm = b_ms[g % 2]
                    P.op("act", lambda e, yb=yb, g4=g4, m_=m_: e.activation(out=m_[:, 0:4], in_=bank[yb][:, g4], func=AF.Abs), reads=[b_bank[yb]], writes=[bm])
                    P.op("dve", lambda e, g4=g4, m_=m_, h=h: e.tensor_tensor(out=m_[:, 4:8], in0=m_[:, 0:4], in1=ENB[:, g4, h], op=ALU.max), reads=[bm, b_G], writes=[bm])
                    P.op("dve", lambda e, m_=m_: e.reciprocal(out=m_[:, 8:12], in_=m_[:, 4:8]), reads=[bm], writes=[bm])
                    hp = hp4[g % 2]
                    for a4 in range(4):
                        xs_ = slice(a4 * 128, (a4 + 1) * 128)
                        bh = b_hp[g % 2][a4]
                        bs_ = b_st[g % 2][a4]
                        P.op("act", lambda e, xb=xb, xs_=xs_, a4=a4, hp=hp, m_=m_: e.activation(out=hp[:, xs_], in_=bank[xb][:, xs_], func=AF.Copy, scale=m_[:, 8 + a4:9 + a4]),
                             reads=[b_bank[xb], bm], writes=[bh])
                        P.op("dve", lambda e, xs_=xs_, a4=a4, hp=hp, m_=m_: e.bn_stats(out=m_[:, 16 + 6 * a4:22 + 6 * a4], in_=hp[:, xs_]), reads=[bh], writes=[bs_])
                        P.op("dve", lambda e, a4=a4, m_=m_: e.bn_aggr(out=m_[:, 40 + 2 * a4:42 + 2 * a4], in_=m_[:, 16 + 6 * a4:22 + 6 * a4]), reads=[bs_], writes=[b_mv[g % 2]])
                    var4 = m_[:, 40:48].rearrange("p (a two) -> p a two", two=2)[:, :, 1]
                    P.op("act", lambda e, m_=m_, var4=var4: e.activation(out=m_[:, 12:16], in_=var4, func=AF.Ln, bias=HEAD_LN_EPS, scale=1.0), reads=[b_mv[g % 2]], writes=[b_rs[g % 2]])
                    P.op("act", lambda e, m_=m_: e.activation(out=m_[:, 12:16], in_=m_[:, 12:16], func=AF.Exp, scale=-0.5), reads=[b_rs[g % 2]], writes=[b_rs[g % 2]])
                def s3_b(g):
                    xb = 6 + (g % 2)
                    m_ = ms4[g % 2]
                    hp = hp4[g % 2]
                    bm = b_ms[g % 2]
                    hn_ = hn4[g % 2]
                    for a4 in range(4):
                        xs_ = slice(a4 * 128, (a4 + 1) * 128)
                        P.op(S3_HN_ENG, lambda e, xs_=xs_, a4=a4, hp=hp, hn_=hn_, m_=m_: e.tensor_scalar(out=hn_[:, xs_], in0=hp[:, xs_], scalar1=m_[:, 40 + 2 * a4:41 + 2 * a4],
                                                                                                 scalar2=m_[:, 12 + a4:13 + a4], op0=ALU.subtract, op1=ALU.mult),
                             reads=[b_hp[g % 2][a4], b_mv[g % 2], b_rs[g % 2]], writes=[b_hn[g % 2]])
                    hbk = next_pb()
                    fns = [(lambda e, a4=a4, hbk=hbk, hn_=hn_: e.matmul(bank[hbk][:, a4 * 128:(a4 + 1) * 128], lhsT=hn_[:, a4 * 128:(a4 + 1) * 128], rhs=ident[:], start=True, stop=True))
                           for a4 in range(4)]
                    P.group("pe", fns, reads=[b_hn[g % 2], b_const], writes=[b_bank[hbk]])
                    gs = slice(g * 512, (g + 1) * 512)
                    P.op("dve", lambda e, hbk=hbk, gs=gs, h=h: e.scalar_tensor_tensor(out=ymT[:, h, gs], in0=bank[hbk][:, 0:512], scalar=cst[:, K_MHW + h:K_MHW + h + 1],
                                                                                    in1=OGZ[:, gs], op0=ALU.mult, op1=ALU.mult),
                         reads=[b_bank[hbk], b_cst, B.ogz], writes=[b_ym[h]])

                NG = NT // 4
                s3_xy(0)
                if NG > 1:
                    s3_xy(1)
                for g in range(NG):
                    s3_pa(g)
                    if g + 2 < NG:
                        s3_xy(g + 2)
                    s3_b(g)

            if stage < 6:
                break
            prefetch(wpa_d, 0); prefetch(win_d, C_GA); prefetch(wpm_d, 0); prefetch(win_d, C_GM)
            P.barrier()
            if dbg and b == 0:
                fresh("dbg")
                for h in range(NH):
                    for g in range(N5):
                        gs = slice(g * 512, (g + 1) * 512)
                        P.op("dve", lambda e, h=h, gs=gs: e.tensor_copy(out=scrF[:, 0:512], in_=yaT[:, h, gs]), reads=[b_ya[h]], writes=[B.dbg])
                        P.dma("sp", lambda e, h=h, gs=gs: e.dma_start(out=dya_d[:, h, gs], in_=scrF[:, 0:512]), "dbg", reads=[B.dbg], writes=[b_out])
                        P.op("dve", lambda e, h=h, gs=gs: e.tensor_copy(out=scrF[:, 0:512], in_=ymT[:, h, gs]), reads=[b_ym[h]], writes=[B.dbg])
                        P.dma("sp", lambda e, h=h, gs=gs: e.dma_start(out=dym_d[:, h, gs], in_=scrF[:, 0:512]), "dbg", reads=[B.dbg], writes=[b_out])
                P.barrier()
            fresh("mg", "wout", "sga", "sgm", "t1", "t2")
            for half in range(2):
                P.dma("pool", lambda e, half=half: e.dma_start(out=woutb[:, half * 4:(half + 1) * 4, :],
                                                              in_=wout_d[half * 512:(half + 1) * 512, :].rearrange("(c p) n -> p c n", p=128)),
                      "wout", writes=[B.wout])
            for jc in range(8):
                wpa_t, wpa_b = get_w(wpa_d, jc * 128)
                wga_t, wga_b = get_w(win_d, C_GA + jc * 128)
                wpm_t, wpm_b = get_w(wpm_d, jc * 128)
                wgm_t, wgm_b = get_w(win_d, C_GM + jc * 128)
                for tt in range(N5):
                    tsl = slice(tt * 512, (tt + 1) * 512)

                    def mmgroup(bk, wt_, wb_, rhsT, rbufs, tsl):
                        fns = [(lambda e, c=c: e.matmul(bank[bk][:, 0:512], lhsT=wt_[:, c, :], rhs=rhsT[:, c, tsl], start=(c == 0), stop=(c == 7))) for c in range(8)]
                        P.group("pe", fns, reads=[wb_] + rbufs, writes=[b_bank[bk]])

                    mmgroup(2, wga_t, wga_b, xnT, [b_xnT], tsl)
                    mmgroup(3, wpa_t, wpa_b, yaT, b_ya, tsl)
                    mmgroup(4, wgm_t, wgm_b, xnT, [b_xnT], tsl)
                    mmgroup(5, wpm_t, wpm_b, ymT, b_ym, tsl)
                    P.op("act", lambda e, jc=jc: e.activation(out=sga, in_=bank[2][:, 0:512], func=AF.Sigmoid, bias=cst[:, K_BG + jc:K_BG + jc + 1], scale=1.0),
                         reads=[b_bank[2], b_cst], writes=[B.sga])
                    P.op("act", lambda e, jc=jc: e.activation(out=sgm, in_=bank[4][:, 0:512], func=AF.Sigmoid, bias=cst[:, K_BG + 8 + jc:K_BG + 8 + jc + 1], scale=1.0),
                         reads=[b_bank[4], b_cst], writes=[B.sgm])
                    P.op("dve", lambda e: e.tensor_tensor(out=t1, in0=bank[3][:, 0:512], in1=sga, op=ALU.mult), reads=[b_bank[3], B.sga], writes=[B.t1])
                    P.op("dve", lambda e: e.tensor_tensor(out=t2, in0=bank[5][:, 0:512], in1=sgm, op=ALU.mult), reads=[b_bank[5], B.sgm], writes=[B.t2])
                    P.op("dve", lambda e, jc=jc, tsl=tsl: e.tensor_tensor(out=mgT[:, jc, tsl], in0=t1, in1=t2, op=ALU.add), reads=[B.t1, B.t2], writes=[B.mg])
            for t in range(NT):
                i = t % NXT
                ts_ = slice(t * 128, (t + 1) * 128)
                P.dma("sp", lambda e, i=i, ts_=ts_, b=b: e.dma_start(out=xt[i][:], in_=x_d[b, ts_, :]), ("x", i), writes=[b_xt[i]])
                for nb in range(2):
                    bk = next_pb()
                    fns = [(lambda e, c=c, bk=bk, nb=nb, ts_=ts_: e.matmul(bank[bk][:, 0:512], lhsT=mgT[:, c, ts_], rhs=woutb[:, c, nb * 512:(nb + 1) * 512],
                                                                        start=(c == 0), stop=(c == 7))) for c in range(8)]
                    P.group("pe", fns, reads=[B.mg, B.wout], writes=[b_bank[bk]])
                    P.op("dve", lambda e, i=i, bk=bk, nb=nb: e.tensor_tensor(out=xt[i][:, nb * 512:(nb + 1) * 512], in0=bank[bk][:, 0:512],
                                                                          in1=xt[i][:, nb * 512:(nb + 1) * 512], op=ALU.add),
                         reads=[b_bank[bk], b_xt[i]], writes=[b_xt[i]])
                P.op("act", lambda e, i=i: e.activation(out=xs[i][:], in_=xt[i][:], func=AF.Square, accum_out=sm[:, 2:3]),
                     reads=[b_xt[i]], writes=[b_xs[i], b_sm])
                rsqrt_act(sm[:, 3:4], sm[:, 2:3], 1.0 / D, NORM_EPS, [b_sm], [b_sm])
                P.op("dve", lambda e, i=i: e.scalar_tensor_tensor(out=xt[i][:], in0=xt[i][:], scalar=sm[:, 3:4], in1=nfb[:], op0=ALU.mult, op1=ALU.mult),
                     reads=[b_xt[i], b_sm, b_cst], writes=[b_xt[i]])
                P.dma("pool", lambda e, i=i, ts_=ts_, b=b: e.dma_start(out=out_d[b, ts_, :], in_=xt[i][:]), "out", reads=[b_xt[i]], writes=[b_out])
            P.barrier()

        P.barrier()
        P.wait_all("sp", [b_out] + b_xt)
        P.emit()
    return nc


def _rel_bucket_np(rel):
    nb = 16
    me = 8
    bucket = np.where(rel > 0, nb, 0)
    n = np.abs(rel)
    nf = np.maximum(n, 1).astype(np.float32)
    large = me + (np.log(nf / np.float32(me)) / np.float32(math.log(128 / me)) * np.float32(nb - me)).astype(np.int32)
    large = np.minimum(large, nb - 1)
    return bucket + np.where(n < me, n, large)


def _host_consts(S, norm_w, lam, subln_w, rel_bias, conv_w, conv_b, b_if, mh_w, b_gate, norm_final):
    f = np.float32
    NT = S // 128
    cst = np.zeros((128, K_TOT), f)
    cst[:, K_NW:K_NW + 8] = norm_w[0].reshape(8, 128).T
    cst[:, K_LAM:K_LAM + 256] = np.broadcast_to(lam[0].reshape(1, 256), (128, 256))
    cst[:, K_SUB] = subln_w[0]
    cst[:, K_CH:K_CH + 8] = np.broadcast_to(rel_bias[15:16, :], (128, 8))
    cw = conv_w[0].reshape(2, 4, 8, 128)
    cst[:, K_CW:K_CW + 64] = cw.transpose(3, 0, 1, 2).reshape(128, 64)
    cb = conv_b[0].reshape(2, 8, 128)
    cst[:, K_CB:K_CB + 16] = cb.transpose(2, 0, 1).reshape(128, 16)
    cst[:, K_MHW:K_MHW + 8] = mh_w[0].reshape(8, 128).T
    bg = b_gate[0].reshape(2, 8, 128)
    cst[:, K_BG:K_BG + 16] = bg.transpose(2, 0, 1).reshape(128, 16)
    bif = np.broadcast_to(b_if[0].reshape(1, 1, 16), (128, NT, 16)).reshape(128, NT * 16).astype(f)
    nfb = np.broadcast_to(norm_final.reshape(1, D), (128, D)).astype(f)
    kk = np.arange(128)[:, None]
    qq = np.arange(256)[None, :]
    biasg = np.zeros((NH, 128, 768), f)
    maskc = np.zeros((128, 768), f)
    for o in range(3):
        krel = (o - 1) * 128 + kk
        rel = krel - qq
        idx = _rel_bucket_np(rel)
        g = rel_bias[idx, :]
        biasg[:, :, o * 256:(o + 1) * 256] = g.transpose(2, 0, 1)
        allowed = (np.floor_divide(krel, 64) <= np.floor_divide(qq, 64))
        maskc[:, o * 256:(o + 1) * 256] = np.where(allowed, 0.0, -1e30)
    return dict(cst=cst, bif=np.ascontiguousarray(bif), nfb=np.ascontiguousarray(nfb), biasg=biasg, maskc=maskc)


_NC_CACHE = {}


def kernel(x, norm_w, w_in, lam, subln_w, rel_bias, conv_w, conv_b, b_if, mh_w, b_gate, w_pa, w_pm, w_out, norm_final):
    x = np.asarray(x, np.float32)
    Bt, S, _ = x.shape
    nseq = Bt // NCORES
    args = [np.asarray(a, np.float32) for a in (norm_w, lam, subln_w, rel_bias, conv_w, conv_b, b_if, mh_w, b_gate, norm_final)]
    consts = _host_consts(S, *args)
    key = (nseq, S)
    if key not in _NC_CACHE:
        _NC_CACHE[key] = build(nseq, S)
    nc = _NC_CACHE[key]
    shared = dict(w_in=np.ascontiguousarray(np.asarray(w_in, np.float32)[0]),
                  w_pa=np.ascontiguousarray(np.asarray(w_pa, np.float32)[0]),
                  w_pm=np.ascontiguousarray(np.asarray(w_pm, np.float32)[0]),
                  w_out=np.ascontiguousarray(np.asarray(w_out, np.float32)[0]), **consts)
    in_maps = []
    for c in range(NCORES):
        m = dict(shared)
        m["x"] = np.ascontiguousarray(x[c * nseq:(c + 1) * nseq])
        in_maps.append(m)
    res = run_bass_kernel_spmd(nc, in_maps, core_ids=list(range(NCORES)))
    out = np.concatenate([np.asarray(r["out"], np.float32) for r in res.results], axis=0)
    return out
```

```python
import math
from contextlib import ExitStack

import numpy as np
import concourse.bass as bass
import concourse.mybir as mybir
from concourse.bass_utils import run_bass_kernel_spmd

F32 = mybir.dt.float32
BF16 = mybir.dt.bfloat16
ALU = mybir.AluOpType
AF = mybir.ActivationFunctionType

D = 1024
NH = 8
D_IN = 11280
NORM_EPS = 1e-6
SUBLN_EPS = 1e-5
HEAD_LN_EPS = 1e-5
LAMBDA_INIT = 0.8 - 0.6 * math.exp(-0.3 * 0)
NCORES = 8
APPROX_RECIP = False
import os
CONV_ENG = tuple(os.environ.get('CONV_ENG', 'dve,dve').split(','))
S3_HP_ENG = os.environ.get('S3_HP_ENG', 'act')
S3_HN_ENG = os.environ.get('S3_HN_ENG', 'dve')
OWN_ENG = tuple(x for x in os.environ.get('OWN_ENG', 'act,dve,pool').split(',') if x)

C_DAQ, C_DAK, C_DAV, C_DAZ = 0, 1024, 2048, 3072
C_MLQ, C_MLK, C_MLV, C_MLO, C_MLZ = 4096, 5120, 6144, 7168, 8192
C_IF = 9216
C_GA, C_GM = 9232, 10256

K_NW, K_LAM, K_SUB, K_CH, K_CW, K_CB, K_MHW, K_BG = 0, 8, 264, 265, 273, 337, 353, 361
K_TOT = 384

ENGS = ["pe", "act", "dve", "pool", "sp"]
EPOCH_MAX = 30000


class Buf:
    __slots__ = ("w", "r", "x")

    def __init__(self, x=False):
        self.w = None
        self.r = []
        self.x = x


class Prog:
    def __init__(self, nc, es):
        self.nc = nc
        self.es = es
        self.q = {e: [] for e in ENGS}
        self.sem = {}
        self.cnt = {e: 0 for e in ENGS}
        self.epoch = {e: 0 for e in ENGS}
        self.waited = {e: {} for e in ENGS}
        self.nsem = 0
        for e in ENGS:
            self.sem[e] = self._newsem(f"s_{e}_0")
        self.dma_sems = {}

    def _newsem(self, name):
        self.nsem += 1
        return self.es.enter_context(self.nc.semaphore(name))

    def _next_token(self, eng):
        if self.cnt[eng] >= EPOCH_MAX:
            self.epoch[eng] += 1
            self.cnt[eng] = 0
            self.sem[eng] = self._newsem(f"s_{eng}_{self.epoch[eng]}")
        self.cnt[eng] += 1
        return (eng, self.epoch[eng], self.cnt[eng], self.sem[eng])

    def _need_wait(self, eng, tok):
        src, ep, val, _ = tok
        cur = self.waited[eng].get(src)
        if cur is not None and cur >= (ep, val):
            return False
        self.waited[eng][src] = (ep, val)
        return True

    def _collect(self, eng, reads, writes, extra=()):
        toks = list(extra)
        for b in reads:
            if b.w is not None:
                toks.append(b.w)
            if b.x:
                toks.extend(t for t in b.r if t[0] != eng)
        for b in writes:
            if b.w is not None:
                toks.append(b.w)
            toks.extend(b.r)
        best = {}
        for t in toks:
            k = t[0]
            if k not in best or (best[k][1], best[k][2]) < (t[1], t[2]):
                best[k] = t
        waits = []
        for k, t in best.items():
            if eng == "pe" and k == "pe":
                continue
            if self._need_wait(eng, t):
                waits.append((t[3], t[2]))
        return waits

    def _commit(self, tok, reads, writes):
        for b in writes:
            b.w = tok
            b.r = []
        for b in reads:
            b.r = [t for t in b.r if t[0] != tok[0]]
            b.r.append(tok)

    def op(self, eng, fn, reads=(), writes=()):
        waits = self._collect(eng, reads, writes)
        tok = self._next_token(eng)
        self.q[eng].append((waits, fn, (tok[3], 1)))
        self._commit(tok, reads, writes)
        return tok

    def group(self, eng, fns, reads=(), writes=()):
        waits = self._collect(eng, reads, writes)
        tok = self._next_token(eng)
        n = len(fns)
        for i, fn in enumerate(fns):
            self.q[eng].append((waits if i == 0 else [], fn, (tok[3], 1) if i == n - 1 else None))
        self._commit(tok, reads, writes)
        return tok

    def dma(self, eng, fn, key, reads=(), writes=()):
        waits = self._collect(eng, reads, writes)
        if key not in self.dma_sems:
            self.dma_sems[key] = [self._newsem(f"d_{len(self.dma_sems)}"), 0]
        ent = self.dma_sems[key]
        ent[1] += 16
        tok = (("dma", key), 0, ent[1], ent[0])
        self.q[eng].append((waits, fn, (ent[0], 16)))
        self._commit(tok, reads, writes)
        return tok

    def barrier(self):
        toks = []
        for e in ENGS:
            if self.cnt[e] > 0 or self.epoch[e] > 0:
                toks.append((e, self.epoch[e], self.cnt[e], self.sem[e]))
        for key, ent in self.dma_sems.items():
            toks.append((("dma", key), 0, ent[1], ent[0]))
        for e in ENGS:
            if not self.q[e]:
                continue
            waits = []
            for t in toks:
                if t[2] == 0 or (t[0] == e and (e not in OWN_ENG)):
                    continue
                if self._need_wait(e, t):
                    waits.append((t[3], t[2]))
            if waits:
                self.q[e].append((waits, None, None))

    def wait_all(self, eng, bufs):
        waits = self._collect(eng, [], bufs)
        self.q[eng].append((waits, None, None))

    def emit(self):
        nc = self.nc
        q = self.q

        def run(engine, lst):
            for waits, fn, sig in lst:
                for sem, val in waits:
                    engine.wait_ge(sem, val)
                if fn is None:
                    continue
                ins = fn(engine)
                if sig is not None:
                    ins.then_inc(sig[0], sig[1])

        with nc.Block() as block:
            if q["pe"]:
                @block.tensor
                def _(e):
                    run(e, q["pe"])
            if q["act"]:
                @block.scalar
                def _(e):
                    run(e, q["act"])
            if q["dve"]:
                @block.vector
                def _(e):
                    run(e, q["dve"])
            if q["pool"]:
                @block.gpsimd
                def _(e):
                    run(e, q["pool"])
            if q["sp"]:
                @block.sync
                def _(e):
                    run(e, q["sp"])


def build(NSEQ, S, dbg=False, stage=99):
    NT = S // 128
    NQ = S // 256
    N5 = S // 512
    NCH = S // 64
    nc = bass.Bass("TRN2", target_bir_lowering=False)
    x_d = nc.dram_tensor("x", [NSEQ, S, D], F32, kind="ExternalInput").ap()
    win_d = nc.dram_tensor("w_in", [D, D_IN], F32, kind="ExternalInput").ap()
    wpa_d = nc.dram_tensor("w_pa", [D, D], F32, kind="ExternalInput").ap()
    wpm_d = nc.dram_tensor("w_pm", [D, D], F32, kind="ExternalInput").ap()
    wout_d = nc.dram_tensor("w_out", [D, D], F32, kind="ExternalInput").ap()
    cst_d = nc.dram_tensor("cst", [128, K_TOT], F32, kind="ExternalInput").ap()
    bif_d = nc.dram_tensor("bif", [128, NT * 16], F32, kind="ExternalInput").ap()
    nfb_d = nc.dram_tensor("nfb", [128, D], F32, kind="ExternalInput").ap()
    biasg_d = nc.dram_tensor("biasg", [NH, 128, 768], F32, kind="ExternalInput").ap()
    maskc_d = nc.dram_tensor("maskc", [128, 768], F32, kind="ExternalInput").ap()
    out_d = nc.dram_tensor("out", [NSEQ, S, D], F32, kind="ExternalOutput").ap()
    if dbg:
        dya_d = nc.dram_tensor("dbg_ya", [128, NH, S], F32, kind="ExternalOutput").ap()
        dym_d = nc.dram_tensor("dbg_ym", [128, NH, S], F32, kind="ExternalOutput").ap()

    with ExitStack() as es:
        P = Prog(nc, es)

        def sb(name, shape, dt):
            return es.enter_context(nc.sbuf_tensor("sb_" + name, shape, dt))

        def ps(name, shape, dt):
            return es.enter_context(nc.psum_tensor("ps_" + name, shape, dt))

        xnT = sb("xnT", [128, 8, S], BF16); b_xnT = Buf()
        yaT = sb("yaT", [128, NH, S], BF16); b_ya = [Buf() for _ in range(NH)]
        ymT = sb("ymT", [128, NH, S], BF16); b_ym = [Buf() for _ in range(NH)]
        NWS = 6
        wsl = [sb(f"wsl{i}", [128, 8, 128], BF16) for i in range(NWS)]
        b_wsl = [Buf() for _ in range(NWS)]
        cst = sb("cst", [128, K_TOT], F32); b_cst = Buf()
        bif = sb("bif", [128, NT * 16], F32)
        nfb = sb("nfb", [128, D], F32)
        maskc = sb("maskc", [128, 768], F32)
        identf = sb("identf", [128, 128], F32)
        ident = sb("ident", [128, 128], BF16)
        ones_b = sb("ones_b", [128, 128], BF16)
        ones_f = sb("ones_f", [128, 128], F32)
        onesA = sb("onesA", [128, 128], F32)
        onesB = sb("onesB", [128, 128], F32)
        tri = sb("tri", [128, 128], F32)
        patA = sb("patA", [128, 128], BF16)
        patB = sb("patB", [128, 128], BF16)
        der = sb("der", [128, 16], F32)
        NXT = 3
        xt = [sb(f"xt{i}", [128, D], F32) for i in range(NXT)]; b_xt = [Buf() for _ in range(NXT)]
        xs = [sb(f"xs{i}", [128, D], BF16) for i in range(NXT)]; b_xs = [Buf() for _ in range(NXT)]
        sm = sb("sm", [128, 16], F32); b_sm = Buf()
        G = sb("G", [128, NT, 16], F32); b_G = Buf()
        LF = sb("LF", [128, NT, 8], F32)
        BC = sb("BC", [128, NT, 8], F32)
        SK = sb("SK", [128, NT, 8], F32)
        ENB = sb("ENB", [128, NT, 8], F32)
        EB = sb("EB", [128, NCH, 8], F32)
        SCB = 24576
        scrB = sb("scrB", [128, SCB], BF16)
        SCF = 3584
        scrF = sb("scrF", [128, SCF], F32)
        kT = scrB[:, 2048:2048 + S]; zs = scrB[:, 4096:4096 + S]
        QQ = scrB[:, 11520:11520 + NQ * 512].rearrange("p (j c) -> p j c", c=512)
        vt = scrB[:, 6144:6144 + NT * 128].rearrange("p (t d) -> p t d", d=128)
        BM = scrB[:, 8192:8192 + 1536].rearrange("p (o c) -> p o c", c=512)
        PT = [scrB[:, 9728 + i * 512: 9728 + (i + 1) * 512] for i in range(3)]
        sq = [scrB[:, 11264:11264 + 256], scrB[:, 17664:17664 + 256]]
        vTf = scrB[:, 15616:15616 + S]
        bgf = scrF[:, 0:768]
        T_ = scrF[:, 768:1280]; rinv = scrF[:, 1280:1792]
        Od = [scrF[:, 1792:2048], scrF[:, 2304:2560]]; lnv = [scrF[:, 2048:2304], scrF[:, 2816:3072]]; yy = [scrF[:, 2560:2816], scrF[:, 3072:3328]]
        mqT = scrB[:, 0:S]; mkT = scrB[:, 2048:2048 + S]
        mv1 = scrB[:, 4096:4096 + NT * 130].rearrange("p (t d) -> p t d", d=130)
        mvTf = scrB[:, 6176:6176 + S]
        OGZ = scrB[:, 8224:8224 + S]; ZG = scrB[:, 10272:10272 + S]
        scTa = scrB[:, 12320:12320 + NT * 128].rearrange("p (t d) -> p t d", d=128)
        kpA = scrB[:, 14368:14368 + NT * 128].rearrange("p (t d) -> p t d", d=128)
        kpB = scrB[:, 16416:16416 + NT * 128].rearrange("p (t d) -> p t d", d=128)
        Cbfa = scrB[:, 18480:18480 + (NCH + 1) * 130].rearrange("p (c d) -> p c d", d=130)
        diagw = scrB[:, 16416:16416 + 1024].rearrange("p (k d) -> p k d", d=128)
        Ubf = [scrB[:, 17440 + i * 516: 17440 + i * 516 + 515] for i in range(2)]
        hn4 = [scrB[:, 22784 + i * 512: 22784 + (i + 1) * 512] for i in range(2)]
        UQt = [scrF[:, i * 515: (i + 1) * 515] for i in range(2)]
        acct = [scrF[:, 1032 + i * 512: 1032 + (i + 1) * 512] for i in range(2)]
        Ub = [scrF[:, 2056 + i * 130: 2056 + i * 130 + 129] for i in range(2)]
        hp4 = [scrF[:, 2320 + i * 512: 2320 + (i + 1) * 512] for i in range(2)]
        ms4 = [scrF[:, 3344 + i * 64: 3344 + (i + 1) * 64] for i in range(2)]
        mgT = scrB[:, 0:8 * S].rearrange("p (c s) -> p c s", s=S)
        woutb = scrB[:, 16384:24576].rearrange("p (c n) -> p c n", n=1024)
        sga = scrF[:, 0:512]; sgm = scrF[:, 512:1024]; t1 = scrF[:, 1024:1536]; t2 = scrF[:, 1536:2048]
        bank = [ps(f"bank{i}", [128, 512], F32) for i in range(8)]
        b_bank = [Buf(x=True) for _ in range(8)]

        class NS:
            pass
        B = NS()

        def fresh(*names):
            for n in names:
                setattr(B, n, Buf())

        b_const = Buf()
        P.dma("sp", lambda e: e.dma_start(out=cst[:], in_=cst_d[:, :]), "c0", writes=[b_cst])
        P.dma("sp", lambda e: e.dma_start(out=bif[:], in_=bif_d[:, :]), "c0", writes=[b_cst])
        P.dma("sp", lambda e: e.dma_start(out=nfb[:], in_=nfb_d[:, :]), "c0", writes=[b_cst])
        P.dma("sp", lambda e: e.dma_start(out=maskc[:], in_=maskc_d[:, :]), "c0", writes=[b_cst])
        P.op("pool", lambda e: e.memset(identf[:], 1.0), writes=[b_const])
        P.op("pool", lambda e: e.affine_select(out=identf[:], in_=identf[:], pattern=[[-1, 128]],
                                                compare_op=ALU.is_equal, fill=0.0, base=0, channel_multiplier=1),
             reads=[b_const], writes=[b_const])
        P.op("pool", lambda e: e.memset(ones_f[:], 1.0), writes=[b_const])
        P.op("pool", lambda e: e.memset(onesA[:], 0.0), writes=[b_const])
        P.op("pool", lambda e: e.memset(onesB[:], 0.0), writes=[b_const])
        P.op("pool", lambda e: e.memset(onesA[0:64, :], 1.0), writes=[b_const])
        P.op("pool", lambda e: e.memset(onesB[64:128, :], 1.0), writes=[b_const])
        P.op("pool", lambda e: e.memset(tri[:], 1.0), writes=[b_const])
        P.op("pool", lambda e: e.affine_select(out=tri[:], in_=tri[:], pattern=[[1, 128]],
                                                compare_op=ALU.is_ge, fill=0.0, base=0, channel_multiplier=-1),
             reads=[b_const], writes=[b_const])
        P.op("pool", lambda e: e.memset(tri[0:64, 64:128], 0.0), reads=[b_const], writes=[b_const])
        P.op("dve", lambda e: e.tensor_copy(out=ident[:], in_=identf[:]), reads=[b_const], writes=[b_const])
        P.op("dve", lambda e: e.memset(ones_b[:], 1.0), writes=[b_const])
        P.op("dve", lambda e: e.memset(patA[:], 0.0), writes=[b_const])
        P.op("dve", lambda e: e.memset(patB[:], 0.0), writes=[b_const])
        P.op("dve", lambda e: e.memset(patA[:, 0:64], 1.0), writes=[b_const])
        P.op("dve", lambda e: e.memset(patB[:, 64:128], 1.0), writes=[b_const])
        P.op("dve", lambda e: e.tensor_tensor(out=cst[:, K_LAM:K_LAM + 64], in0=cst[:, K_LAM:K_LAM + 64],
                                              in1=cst[:, K_LAM + 64:K_LAM + 128], op=ALU.mult), reads=[b_cst], writes=[b_cst])
        P.op("dve", lambda e: e.tensor_tensor(out=cst[:, K_LAM + 128:K_LAM + 192], in0=cst[:, K_LAM + 128:K_LAM + 192],
                                              in1=cst[:, K_LAM + 192:K_LAM + 256], op=ALU.mult), reads=[b_cst], writes=[b_cst])
        P.op("dve", lambda e: e.tensor_reduce(out=der[:, 2:3], in_=cst[:, K_LAM:K_LAM + 64], axis=mybir.AxisListType.X, op=ALU.add),
             reads=[b_cst], writes=[b_const])
        P.op("dve", lambda e: e.tensor_reduce(out=der[:, 3:4], in_=cst[:, K_LAM + 128:K_LAM + 192], axis=mybir.AxisListType.X, op=ALU.add),
             reads=[b_cst], writes=[b_const])
        P.op("act", lambda e: e.activation(out=der[:, 2:4], in_=der[:, 2:4], func=AF.Exp), reads=[b_const], writes=[b_const])
        P.op("dve", lambda e: e.scalar_tensor_tensor(out=der[:, 0:1], in0=der[:, 3:4], scalar=-LAMBDA_INIT, in1=der[:, 2:3],
                                                     op0=ALU.add, op1=ALU.subtract), reads=[b_const], writes=[b_const])
        P.op("dve", lambda e: e.tensor_scalar(out=der[:, 1:2], in0=cst[:, K_SUB:K_SUB + 1], scalar1=(1.0 - LAMBDA_INIT), scalar2=None,
                                              op0=ALU.mult), reads=[b_cst, b_const], writes=[b_const])

        P.op("dve", lambda e: e.tensor_scalar(out=der[:, 8:16], in0=cst[:, K_CH:K_CH + 8], scalar1=-1.0, scalar2=None, op0=ALU.mult),
             reads=[b_cst, b_const], writes=[b_const])
        state = {"w": 0, "pb": 0}

        def load_wblock(w_ap, col0, ncols=128):
            i = state["w"] % NWS
            state["w"] += 1
            src = w_ap[:, col0:col0 + ncols].rearrange("(c p) n -> p c n", p=128)
            P.dma("pool", lambda e: e.dma_start(out=wsl[i][:, :, 0:ncols], in_=src), ("w", i), writes=[b_wsl[i]])
            return wsl[i], b_wsl[i]

        pref = {}

        def prefetch(w_ap, col0, ncols=128):
            pref[(id(w_ap), col0)] = load_wblock(w_ap, col0, ncols)

        def get_w(w_ap, col0, ncols=128):
            k = (id(w_ap), col0)
            if k in pref:
                return pref.pop(k)
            return load_wblock(w_ap, col0, ncols)

        def next_pb():
            k = state["pb"] % 2
            state["pb"] += 1
            return k

        def proj_fm(wt, wb, rhsT, rhs_bufs, evac):
            for tt in range(N5):
                bk = next_pb()
                fns = [(lambda e, c=c, bk=bk, tt=tt: e.matmul(bank[bk][:, 0:512], lhsT=wt[:, c, :],
                                                             rhs=rhsT[:, c, tt * 512:(tt + 1) * 512],
                                                             start=(c == 0), stop=(c == 7))) for c in range(8)]
                P.group("pe", fns, reads=[wb] + rhs_bufs, writes=[b_bank[bk]])
                evac(tt, bank[bk], b_bank[bk])

        def proj_tm(wt, wb, evac):
            for g in range(NT // 4):
                bk = next_pb()
                for a in range(4):
                    t = g * 4 + a
                    fns = [(lambda e, c=c, bk=bk, a=a, t=t: e.matmul(bank[bk][:, a * 128:(a + 1) * 128],
                                                                   lhsT=xnT[:, c, t * 128:(t + 1) * 128], rhs=wt[:, c, :],
                                                                   start=(c == 0), stop=(c == 7))) for c in range(8)]
                    P.group("pe", fns, reads=[wb, b_xnT], writes=[b_bank[bk]])
                evac(g, bank[bk], b_bank[bk])

        def proj_v1(wt, wb, vT_ap, vT_buf):
            proj_fm(wt, wb, xnT, [b_xnT], lambda tt, bkap, bb: P.op(
                "dve", lambda e: e.tensor_copy(out=vT_ap[:, tt * 512:(tt + 1) * 512], in_=bkap[:, 0:512]), reads=[bb], writes=[vT_buf]))

        def proj_v2(vT_ap, vT_buf, evac):
            for g in range(NT // 4):
                bk = next_pb()
                fns = [(lambda e, a=a, bk=bk, g=g: e.matmul(bank[bk][:, a * 128:(a + 1) * 128], lhsT=vT_ap[:, (g * 4 + a) * 128:(g * 4 + a + 1) * 128],
                                                          rhs=ident[:], start=True, stop=True)) for a in range(4)]
                P.group("pe", fns, reads=[vT_buf, b_const], writes=[b_bank[bk]])
                evac(g, bank[bk], b_bank[bk])

        def rsqrt_act(dst, src, scale, eps, rbufs, wbufs):
            P.op("act", lambda e: e.activation(out=dst, in_=src, func=AF.Ln, bias=eps, scale=scale), reads=rbufs, writes=wbufs)
            P.op("act", lambda e: e.activation(out=dst, in_=dst, func=AF.Exp, scale=-0.5), reads=wbufs, writes=wbufs)

        b_out = Buf()
        b_bgf = Buf()

        for b in range(NSEQ):
            if stage < 2:
                break
            for t in range(NT):
                i = t % NXT
                P.dma("sp", lambda e, i=i, t=t, b=b: e.dma_start(out=xt[i][:], in_=x_d[b, t * 128:(t + 1) * 128, :]), ("x", i), writes=[b_xt[i]])
                P.op("act", lambda e, i=i: e.activation(out=xs[i][:], in_=xt[i][:], func=AF.Square, accum_out=sm[:, 0:1]),
                     reads=[b_xt[i]], writes=[b_xs[i], b_sm])
                rsqrt_act(sm[:, 1:2], sm[:, 0:1], 1.0 / D, NORM_EPS, [b_sm], [b_sm])
                P.op("act", lambda e, i=i: e.activation(out=xs[i][:], in_=xt[i][:], func=AF.Copy, scale=sm[:, 1:2]),
                     reads=[b_xt[i], b_sm], writes=[b_xs[i]])
                for half in range(2):
                    bk = next_pb()
                    fns = [(lambda e, a=a, bk=bk, i=i, half=half: e.matmul(bank[bk][:, a * 128:(a + 1) * 128],
                                                                          lhsT=xs[i][:, (half * 4 + a) * 128:(half * 4 + a + 1) * 128],
                                                                          rhs=ident[:], start=True, stop=True)) for a in range(4)]
                    P.group("pe", fns, reads=[b_xs[i], b_const], writes=[b_bank[bk]])
                    for a in range(4):
                        c = half * 4 + a
                        P.op("dve", lambda e, a=a, c=c, bk=bk, t=t: e.tensor_scalar(out=xnT[:, c, t * 128:(t + 1) * 128],
                                                                                 in0=bank[bk][:, a * 128:(a + 1) * 128],
                                                                                 scalar1=cst[:, K_NW + c:K_NW + c + 1], scalar2=None, op0=ALU.mult),
                             reads=[b_bank[bk], b_cst], writes=[b_xnT])

            if stage < 3:
                break
            wt, wb = load_wblock(win_d, C_IF, 16)
            bk = next_pb()
            for t in range(NT):
                fns = [(lambda e, c=c, t=t, bk=bk, wt=wt: e.matmul(bank[bk][:, t * 16:(t + 1) * 16], lhsT=xnT[:, c, t * 128:(t + 1) * 128],
                                                                 rhs=wt[:, c, 0:16], start=(c == 0), stop=(c == 7))) for c in range(8)]
                P.group("pe", fns, reads=[wb, b_xnT], writes=[b_bank[bk]])
            Gf = G[:].rearrange("p t g -> p (t g)")
            P.op("dve", lambda e, bk=bk: e.tensor_tensor(out=Gf, in0=bank[bk][:, 0:NT * 16], in1=bif[:], op=ALU.add),
                 reads=[b_bank[bk], b_cst], writes=[b_G])
            P.op("act", lambda e: e.activation(out=LF[:], in_=G[:, :, 8:16], func=AF.Exp, scale=-1.0), reads=[b_G], writes=[b_G])
            P.op("act", lambda e: e.activation(out=LF[:], in_=LF[:], func=AF.Ln, bias=1.0, scale=1.0), reads=[b_G], writes=[b_G])
            P.op("dve", lambda e: e.tensor_scalar(out=LF[:], in0=LF[:], scalar1=-1.0, scalar2=None, op0=ALU.mult), reads=[b_G], writes=[b_G])
            bk = next_pb()
            for t in range(NT):
                P.op("pe", lambda e, t=t, bk=bk: e.matmul(bank[bk][:, t * 8:(t + 1) * 8], lhsT=tri[:], rhs=LF[:, t, :], start=True, stop=True),
                     reads=[b_G, b_const], writes=[b_bank[bk]])
            P.op("dve", lambda e, bk=bk: e.tensor_copy(out=BC[:].rearrange("p t g -> p (t g)"), in_=bank[bk][:, 0:NT * 8]),
                 reads=[b_bank[bk]], writes=[b_G])
            bk = next_pb()
            for t in range(NT):
                for hf in range(2):
                    c = 2 * t + hf
                    P.op("pe", lambda e, t=t, hf=hf, c=c, bk=bk: e.matmul(bank[bk][:, c * 8:(c + 1) * 8], lhsT=(onesA if hf == 0 else onesB)[:],
                                                                         rhs=LF[:, t, :], start=True, stop=True),
                         reads=[b_G, b_const], writes=[b_bank[bk]])
            P.op("act", lambda e, bk=bk: e.activation(out=EB[:].rearrange("p c g -> p (c g)"), in_=bank[bk][:, 0:NCH * 8], func=AF.Exp),
                 reads=[b_bank[bk]], writes=[b_G])
            P.op("dve", lambda e: e.tensor_tensor(out=SK[:], in0=G[:, :, 0:8], in1=BC[:], op=ALU.subtract), reads=[b_G], writes=[b_G])
            P.op("dve", lambda e: e.tensor_scalar(out=SK[:], in0=SK[:], scalar1=-0.5 * math.log(128.0), scalar2=None, op0=ALU.add),
                 reads=[b_G], writes=[b_G])
            P.op("act", lambda e: e.activation(out=SK[:], in_=SK[:], func=AF.Exp), reads=[b_G], writes=[b_G])
            P.op("act", lambda e: e.activation(out=ENB[:], in_=BC[:], func=AF.Exp, scale=-1.0), reads=[b_G], writes=[b_G])

            if stage < 4:
                break
            for h in range(NH):
                for cc in (C_DAV, C_DAQ, C_DAK, C_DAZ):
                    prefetch(win_d, cc + h * 128)
                P.barrier()
                fresh("qT", "kT", "zs", "vt", "vTf", "BM", "T", "rinv")
                b_Od = [Buf() for _ in range(2)]; b_sq = [Buf() for _ in range(2)]; b_lnv = [Buf() for _ in range(2)]; b_yy = [Buf() for _ in range(2)]
                b_PT = [Buf() for _ in range(3)]
                P.op("dve", lambda e: e.memset(QQ[64:128, :, 0:256], 0.0), writes=[B.qT])
                P.op("dve", lambda e: e.memset(QQ[0:64, :, 256:512], 0.0), writes=[B.qT])
                if h == 0:
                    P.dma("sp", lambda e, h=h: e.dma_start(out=bgf, in_=biasg_d[h, :, :]), "bg", writes=[b_bgf])
                for o in range(3):
                    for dup in range(2):
                        P.op("dve", lambda e, o=o, dup=dup, h=h: e.scalar_tensor_tensor(out=BM[:, o, dup * 256:(dup + 1) * 256], in0=bgf[:, o * 256:(o + 1) * 256],
                                                                                      scalar=der[:, 8 + h:9 + h], in1=maskc[:, o * 256:(o + 1) * 256],
                                                                                      op0=ALU.add, op1=ALU.add),
                             reads=[b_bgf, b_cst, b_const], writes=[B.BM])
                wt, wb = get_w(win_d, C_DAV + h * 128)
                proj_v1(wt, wb, vTf, B.vTf)
                wt, wb = get_w(win_d, C_DAQ + h * 128)
                if h + 1 < NH:
                    P.dma("sp", lambda e, h=h: e.dma_start(out=bgf, in_=biasg_d[h + 1, :, :]), "bg", reads=[], writes=[b_bgf])

                def q_evac(tt, bkap, bb):
                    for part in range(2):
                        pr = slice(part * 64, (part + 1) * 64)
                        P.op("act", lambda e, pr=pr, part=part: e.activation(
                            out=QQ[pr, 2 * tt:2 * tt + 2, part * 256:(part + 1) * 256],
                            in_=bkap[pr, 0:512].rearrange("p (a c) -> p a c", a=2), func=AF.Copy, scale=0.125),
                            reads=[bb], writes=[B.qT])
                proj_fm(wt, wb, xnT, [b_xnT], q_evac)
                wt, wb = get_w(win_d, C_DAK + h * 128)
                proj_fm(wt, wb, xnT, [b_xnT], lambda tt, bkap, bb: P.op(
                    "dve", lambda e: e.tensor_copy(out=kT[:, tt * 512:(tt + 1) * 512], in_=bkap[:, 0:512]),
                    reads=[bb], writes=[B.kT]))
                wt, wb = get_w(win_d, C_DAZ + h * 128)
                proj_fm(wt, wb, xnT, [b_xnT], lambda tt, bkap, bb: P.op(
                    "act", lambda e: e.activation(out=zs[:, tt * 512:(tt + 1) * 512], in_=bkap[:, 0:512], func=AF.Silu),
                    reads=[bb], writes=[B.zs]))
                proj_v2(vTf, B.vTf, lambda g, bkap, bb: P.op(
                    "dve", lambda e: e.tensor_copy(out=vt[:, g * 4:(g + 1) * 4, :], in_=bkap[:, 0:512].rearrange("p (a d) -> p a d", a=4)),
                    reads=[bb], writes=[B.vt]))

                pairs = [(j, kt) for j in range(NQ) for kt in range(2 * j + 2)]

                def emit_S(i):
                    j, kt = pairs[i]
                    sbk = 2 + (i % 2)
                    near = kt >= 2 * j - 1
                    fns = []
                    rd = [B.kT, B.qT]
                    if near:
                        o = kt - 2 * j + 1
                        fns.append(lambda e: e.matmul(bank[sbk][:, 0:512], lhsT=ident[:], rhs=BM[:, o, :], start=True, stop=False))
                        rd = rd + [B.BM, b_const]
                    fns.append(lambda e: e.matmul(bank[sbk][:, 0:512], lhsT=kT[:, kt * 128:(kt + 1) * 128], rhs=QQ[:, j, :],
                                                  start=(not near), stop=True))
                    P.group("pe", fns, reads=rd, writes=[b_bank[sbk]])

                pending = []

                def epilogue2_pe(pj):
                    k2 = pj % 2
                    sbk2 = next_pb()
                    P.op("pe", lambda e, sbk2=sbk2: e.matmul(bank[sbk2][:, 0:256], lhsT=ones_b[:], rhs=sq[k2], start=True, stop=True),
                         reads=[b_sq[k2], b_const], writes=[b_bank[sbk2]])
                    return sbk2

                def epilogue2(pj, ph, sbk2=None):
                    qs = slice(pj * 256, (pj + 1) * 256)
                    k2 = pj % 2
                    if sbk2 is None:
                        sbk2 = epilogue2_pe(pj)
                    rsqrt_act(lnv[k2], bank[sbk2][:, 0:256], 1.0 / 128.0, SUBLN_EPS, [b_bank[sbk2]], [b_lnv[k2]])
                    P.op("dve", lambda e: e.scalar_tensor_tensor(out=yy[k2], in0=Od[k2], scalar=der[:, 1:2], in1=lnv[k2], op0=ALU.mult, op1=ALU.mult),
                         reads=[b_Od[k2], b_lnv[k2], b_const], writes=[b_yy[k2]])
                    P.op("dve", lambda e, qs=qs, ph=ph: e.tensor_tensor(out=yaT[:, ph, qs], in0=yy[k2], in1=zs[:, qs], op=ALU.mult),
                         reads=[b_yy[k2], B.zs], writes=[b_ya[ph]])

                emit_S(0)
                for i, (j, kt) in enumerate(pairs):
                    if i + 1 < len(pairs):
                        emit_S(i + 1)
                    sbk = 2 + (i % 2)
                    pt = i % 3
                    near = kt >= 2 * j - 1
                    ss_bank = None
                    if kt == 2 * j + 1 and pending:
                        ss_bank = epilogue2_pe(pending[0][0])
                    P.op("act", lambda e, sbk=sbk, pt=pt: e.activation(out=PT[pt], in_=bank[sbk][:, 0:512], func=AF.Exp),
                         reads=[b_bank[sbk]], writes=[b_PT[pt]])
                    first = (kt == 0)
                    last = (kt == 2 * j + 1)
                    ob = 4 + 2 * (j % 2)
                    rb = ob + 1
                    P.op("pe", lambda e, kt=kt, pt=pt, first=first, last=last, ob=ob: e.matmul(bank[ob][:, 0:512], lhsT=vt[:, kt, :], rhs=PT[pt],
                                                                                              start=first, stop=last),
                         reads=[B.vt, b_PT[pt]], writes=[b_bank[ob]])
                    P.op("pe", lambda e, pt=pt, first=first, last=last, rb=rb: e.matmul(bank[rb][:, 0:512], lhsT=ones_b[:], rhs=PT[pt],
                                                                                       start=first, stop=last),
                         reads=[b_const, b_PT[pt]], writes=[b_bank[rb]])
                    if last:
                        qs = slice(j * 256, (j + 1) * 256)
                        if APPROX_RECIP:
                            P.op("dve", lambda e, rb=rb: e.reciprocal_approx_fast(out=rinv, in_=bank[rb][:, 0:512]), reads=[b_bank[rb]], writes=[B.rinv])
                        else:
                            P.op("dve", lambda e, rb=rb: e.reciprocal(out=rinv, in_=bank[rb][:, 0:512]), reads=[b_bank[rb]], writes=[B.rinv])
                        P.op("dve", lambda e, ob=ob: e.tensor_tensor(out=T_, in0=bank[ob][:, 0:512], in1=rinv, op=ALU.mult),
                             reads=[b_bank[ob], B.rinv], writes=[B.T])
                        k1 = j % 2
                        P.op("dve", lambda e, k1=k1: e.scalar_tensor_tensor(out=Od[k1], in0=T_[:, 256:512], scalar=der[:, 0:1], in1=T_[:, 0:256],
                                                                            op0=ALU.mult, op1=ALU.add), reads=[B.T, b_const], writes=[b_Od[k1]])
                        P.op("dve", lambda e, k1=k1: e.tensor_tensor(out=sq[k1], in0=Od[k1], in1=Od[k1], op=ALU.mult), reads=[b_Od[k1]], writes=[b_sq[k1]])
                        if pending:
                            pj, ph = pending.pop(0)
                            epilogue2(pj, ph, ss_bank)
                        pending.append((j, h))
                while pending:
                    pj, ph = pending.pop(0)
                    epilogue2(pj, ph)

                for cc in (C_MLV, C_MLQ, C_MLK):
                    prefetch(win_d, cc + h * 128)
                P.barrier()
                fresh("mq", "mk", "mv", "vTf", "ogz", "zg", "scT", "kp")
                b_UQ = [Buf() for _ in range(2)]; b_acc = [Buf() for _ in range(2)]; b_U = [Buf() for _ in range(2)]
                b_Cbf = [Buf() for _ in range(NCH // 4 + 1)]
                b_hp = [[Buf() for _ in range(4)] for _ in range(2)]; b_hn = [Buf() for _ in range(2)]; b_ms = [Buf() for _ in range(2)]
                b_st = [[Buf() for _ in range(4)] for _ in range(2)]; b_mv = [Buf() for _ in range(2)]; b_rs = [Buf() for _ in range(2)]
                P.op("dve", lambda e: e.memset(mv1[:, :, 128:130], 1.0), writes=[B.mv])
                P.op("dve", lambda e: e.memset(Cbfa[:, 0, :], 0.0), writes=[b_Cbf[0]])
                wt, wb = get_w(win_d, C_MLV + h * 128)
                proj_v1(wt, wb, mvTf, B.vTf)
                b_dg = Buf()
                for qk in range(2):
                    for j in range(4):
                        cwj = cst[:, K_CW + (qk * 4 + j) * 8 + h:K_CW + (qk * 4 + j) * 8 + h + 1]
                        P.op("dve", lambda e, qk=qk, j=j, cwj=cwj: e.tensor_scalar(out=diagw[:, qk * 4 + j, :], in0=identf[:], scalar1=cwj, scalar2=None, op0=ALU.mult),
                             reads=[b_const, b_cst], writes=[b_dg])
                uctr = 0
                for qk, col0, dstT in ((0, C_MLQ, mqT), (1, C_MLK, mkT)):
                    wt, wb = get_w(win_d, col0 + h * 128)
                    cb = cst[:, K_CB + qk * 8 + h:K_CB + qk * 8 + h + 1]
                    dbuf = B.mq if qk == 0 else B.mk
                    P.op("dve", lambda e, u=uctr % 2: e.memset(Ubf[u][:, 0:3], 0.0), writes=[b_UQ[uctr % 2]])
                    pend = []

                    def conv_emit(tt, u, qk=qk, cb=cb, dstT=dstT, dbuf=dbuf):
                        cbk = 2 + (tt % 2)
                        fns = [(lambda e, j=j: e.matmul(bank[cbk][:, 0:512], lhsT=diagw[:, qk * 4 + j, :], rhs=Ubf[u][:, j:j + 512], start=(j == 0), stop=(j == 3)))
                               for j in range(4)]
                        P.group("pe", fns, reads=[b_dg, b_UQ[u]], writes=[b_bank[cbk]])
                        P.op("act", lambda e: e.activation(out=dstT[:, tt * 512:(tt + 1) * 512], in_=bank[cbk][:, 0:512], func=AF.Silu, bias=cb, scale=1.0),
                             reads=[b_bank[cbk], b_cst], writes=[dbuf])

                    def qk_evac(tt, bkap, bb, pend=pend, conv_emit=conv_emit):
                        nonlocal uctr
                        u = uctr % 2
                        uctr += 1
                        P.op("act", lambda e: e.activation(out=Ubf[u][:, 3:515], in_=bkap[:, 0:512], func=AF.Copy), reads=[bb], writes=[b_UQ[u]])
                        if pend:
                            conv_emit(*pend.pop())
                        if tt < N5 - 1:
                            P.op("dve", lambda e: e.tensor_copy(out=Ubf[1 - u][:, 0:3], in_=Ubf[u][:, 512:515]), reads=[b_UQ[u]], writes=[b_UQ[1 - u]])
                        pend.append((tt, u))
                    proj_fm(wt, wb, xnT, [b_xnT], qk_evac)
                    conv_emit(*pend.pop())
                proj_v2(mvTf, B.vTf, lambda g, bkap, bb: P.op(
                    "dve", lambda e: e.tensor_copy(out=mv1[:, g * 4:(g + 1) * 4, 0:128], in_=bkap[:, 0:512].rearrange("p (a d) -> p a d", a=4)),
                    reads=[bb], writes=[B.mv]))
                for t in range(NT):
                    ts_ = slice(t * 128, (t + 1) * 128)
                    sbk = 2 + (t % 2)
                    kbk = 4 + (t % 2)
                    P.op("pe", lambda e, ts_=ts_, sbk=sbk: e.matmul(bank[sbk][:, 0:128], lhsT=mkT[:, ts_], rhs=mqT[:, ts_], start=True, stop=True),
                         reads=[B.mk, B.mq], writes=[b_bank[sbk]])
                    P.op("pe", lambda e, ts_=ts_, kbk=kbk: e.matmul(bank[kbk][:, 0:128], lhsT=mkT[:, ts_], rhs=ident[:], start=True, stop=True),
                         reads=[B.mk, b_const], writes=[b_bank[kbk]])
                    P.op("dve", lambda e, sbk=sbk, t=t, h=h: e.scalar_tensor_tensor(out=scTa[:, t, :], in0=bank[sbk][:, 0:128], scalar=SK[:, t, h:h + 1], in1=tri[:],
                                                                                  op0=ALU.mult, op1=ALU.mult),
                         reads=[b_bank[sbk], b_G, b_const], writes=[B.scT])
                    P.op("act", lambda e, kbk=kbk, t=t, h=h: e.activation(out=kpA[:, t, :], in_=bank[kbk][:, 0:128], func=AF.Copy, scale=SK[:, t, h:h + 1]),
                         reads=[b_bank[kbk], b_G], writes=[B.kp])
                zo_jobs = []
                wz_t, wz_b = get_w(win_d, C_MLZ + h * 128)
                wo_t, wo_b = get_w(win_d, C_MLO + h * 128)

                def zo_group(kind, tt):
                    wt_, wb_ = (wz_t, wz_b) if kind == "z" else (wo_t, wo_b)
                    bk = next_pb()
                    fns = [(lambda e, c=c: e.matmul(bank[bk][:, 0:512], lhsT=wt_[:, c, :], rhs=xnT[:, c, tt * 512:(tt + 1) * 512], start=(c == 0), stop=(c == 7)))
                           for c in range(8)]
                    P.group("pe", fns, reads=[wb_, b_xnT], writes=[b_bank[bk]])
                    sl = slice(tt * 512, (tt + 1) * 512)
                    if kind == "z":
                        P.op("act", lambda e: e.activation(out=ZG[:, sl], in_=bank[bk][:, 0:512], func=AF.Silu), reads=[b_bank[bk]], writes=[B.zg])
                    else:
                        P.op("act", lambda e: e.activation(out=OGZ[:, sl], in_=bank[bk][:, 0:512], func=AF.Sigmoid), reads=[b_bank[bk]], writes=[B.ogz])
                        P.op("dve", lambda e: e.tensor_tensor(out=OGZ[:, sl], in0=OGZ[:, sl], in1=ZG[:, sl], op=ALU.mult), reads=[B.ogz, B.zg], writes=[B.ogz])
                for kind in ("z", "o"):
                    for tt in range(N5):
                        zo_jobs.append((kind, tt))
                dbanks = (6, 7, 4, 5)
                for c in range(NCH):
                    t, hf = c // 2, c % 2
                    db = dbanks[c % 4]
                    ps_ = slice(hf * 64, (hf + 1) * 64)
                    P.op("pe", lambda e, db=db, ps_=ps_, t=t: e.matmul(bank[db][:, 0:129], lhsT=kpA[ps_, t, :], rhs=mv1[ps_, t, 0:129], start=True, stop=True),
                         reads=[B.kp, B.mv], writes=[b_bank[db]])
                    u = c % 2
                    if c == 0:
                        P.op("dve", lambda e, db=db: e.tensor_copy(out=Ub[0], in_=bank[db][:, 0:129]), reads=[b_bank[db]], writes=[b_U[0]])
                    else:
                        P.op("dve", lambda e, db=db, u=u, c=c, h=h: e.scalar_tensor_tensor(out=Ub[u], in0=Ub[1 - u], scalar=EB[:, c - 1, h:h + 1], in1=bank[db][:, 0:129],
                                                                                        op0=ALU.mult, op1=ALU.add),
                             reads=[b_bank[db], b_G, b_U[1 - u]], writes=[b_U[u]])
                    if c < NCH - 1:
                        P.op("act", lambda e, u=u, c=c, h=h: e.activation(out=Cbfa[:, c + 1, 0:129], in_=Ub[u], func=AF.Copy, scale=EB[:, c, h:h + 1]),
                             reads=[b_U[u], b_G], writes=[b_Cbf[(c + 1) // 4]])
                    if c % 4 == 1 and zo_jobs:
                        zo_group(*zo_jobs.pop(0))
                while zo_jobs:
                    zo_group(*zo_jobs.pop(0))
                def s3_xy(g):
                    xb = 6 + (g % 2)
                    yb = 2 + (g % 2)
                    m_ = ms4[g % 2]
                    for a4 in range(4):
                        t = g * 4 + a4
                        c0 = 2 * t
                        xs_ = slice(a4 * 128, (a4 + 1) * 128)
                        for (osl, rs) in ((xs_, slice(0, 128)), (slice(t, t + 1), slice(128, 129))):
                            ob_ = xb if rs.start == 0 else yb
                            fns = [lambda e, c0=c0, ob_=ob_, osl=osl, rs=rs: e.matmul(bank[ob_][0:64, osl], lhsT=mqT[:, c0 * 64:(c0 + 1) * 64], rhs=Cbfa[:, c0, rs],
                                                                                    start=True, stop=False, tile_position=(0, 0)),
                                   lambda e, c0=c0, ob_=ob_, osl=osl, rs=rs: e.matmul(bank[ob_][64:128, osl], lhsT=mqT[:, (c0 + 1) * 64:(c0 + 2) * 64], rhs=Cbfa[:, c0 + 1, rs],
                                                                                    start=True, stop=False, tile_position=(0, 64)),
                                   lambda e, t=t, ob_=ob_, osl=osl, rs=rs: e.matmul(bank[ob_][:, osl], lhsT=scTa[:, t, :], rhs=mv1[:, t, rs], start=False, stop=True)]
                            rd = [B.scT, B.mv, B.mq, b_Cbf[c0 // 4], b_Cbf[(c0 + 1) // 4]]
                            P.group("pe", fns, reads=rd, writes=[b_bank[ob_]])
                def s3_pa(g):
                    xb = 6 + (g % 2)
                    yb = 2 + (g % 2)
                    m_ = ms4[g % 2]
                    g4 = slice(g * 4, (g + 1) * 4)
                    bm = b_ms[g % 2]
                    P.op("act", lambda e, yb=yb, g4=g4, m_=m_: e.activation(out=m_[:, 0:4], in_=bank[yb][:, g4], func=AF.Abs), reads=[b_bank[yb]], writes=[bm])
                    P.op("dve", lambda e, g4=g4, m_=m_, h=h: e.tensor_tensor(out=m_[:, 4:8], in0=m_[:, 0:4], in1=ENB[:, g4, h], op=ALU.max), reads=[bm, b_G], writes=[bm])
                    P.op("dve", lambda e, m_=m_: e.reciprocal(out=m_[:, 8:12], in_=m_[:, 4:8]), reads=[bm], writes=[bm])
                    hp = hp4[g % 2]
                    for a4 in range(4):
                        xs_ = slice(a4 * 128, (a4 + 1) * 128)
                        bh = b_hp[g % 2][a4]
                        bs_ = b_st[g % 2][a4]
                        P.op("act", lambda e, xb=xb, xs_=xs_, a4=a4, hp=hp, m_=m_: e.activation(out=hp[:, xs_], in_=bank[xb][:, xs_], func=AF.Copy, scale=m_[:, 8 + a4:9 + a4]),
                             reads=[b_bank[xb], bm], writes=[bh])
                        P.op("dve", lambda e, xs_=xs_, a4=a4, hp=hp, m_=m_: e.bn_stats(out=m_[:, 16 + 6 * a4:22 + 6 * a4], in_=hp[:, xs_]), reads=[bh], writes=[bs_])
                        P.op("dve", lambda e, a4=a4, m_=m_: e.bn_aggr(out=m_[:, 40 + 2 * a4:42 + 2 * a4], in_=m_[:, 16 + 6 * a4:22 + 6 * a4]), reads=[bs_], writes=[b_mv[g % 2]])
                    var4 = m_[:, 40:48].rearrange("p (a two) -> p a two", two=2)[:, :, 1]
                    P.op("act", lambda e, m_=m_, var4=var4: e.activation(out=m_[:, 12:16], in_=var4, func=AF.Ln, bias=HEAD_LN_EPS, scale=1.0), reads=[b_mv[g % 2]], writes=[b_rs[g % 2]])
                    P.op("act", lambda e, m_=m_: e.activation(out=m_[:, 12:16], in_=m_[:, 12:16], func=AF.Exp, scale=-0.5), reads=[b_rs[g % 2]], writes=[b_rs[g % 2]])
                def s3_b(g):
                    xb = 6 + (g % 2)
                    m_ = ms4[g % 2]
                    hp = hp4[g % 2]
                    bm = b_ms[g % 2]
                    hn_ = hn4[g % 2]
                    for a4 in range(4):
                        xs_ = slice(a4 * 128, (a4 + 1) * 128)
                        P.op(S3_HN_ENG, lambda e, xs_=xs_, a4=a4, hp=hp, hn_=hn_, m_=m_: e.tensor_scalar(out=hn_[:, xs_], in0=hp[:, xs_], scalar1=m_[:, 40 + 2 * a4:41 + 2 * a4],
                                                                                                 scalar2=m_[:, 12 + a4:13 + a4], op0=ALU.subtract, op1=ALU.mult),
                             reads=[b_hp[g % 2][a4], b_mv[g % 2], b_rs[g % 2]], writes=[b_hn[g % 2]])
                    hbk = next_pb()
                    fns = [(lambda e, a4=a4, hbk=hbk, hn_=hn_: e.matmul(bank[hbk][:, a4 * 128:(a4 + 1) * 128], lhsT=hn_[:, a4 * 128:(a4 + 1) * 128], rhs=ident[:], start=True, stop=True))
                           for a4 in range(4)]
                    P.group("pe", fns, reads=[b_hn[g % 2], b_const], writes=[b_bank[hbk]])
                    gs = slice(g * 512, (g + 1) * 512)
                    P.op("dve", lambda e, hbk=hbk, gs=gs, h=h: e.scalar_tensor_tensor(out=ymT[:, h, gs], in0=bank[hbk][:, 0:512], scalar=cst[:, K_MHW + h:K_MHW + h + 1],
                                                                                    in1=OGZ[:, gs], op0=ALU.mult, op1=ALU.mult),
                         reads=[b_bank[hbk], b_cst, B.ogz], writes=[b_ym[h]])

                NG = NT // 4
                s3_xy(0)
                if NG > 1:
                    s3_xy(1)
                for g in range(NG):
                    s3_pa(g)
                    if g + 2 < NG:
                        s3_xy(g + 2)
                    s3_b(g)

            if stage < 6:
                break
            prefetch(wpa_d, 0); prefetch(win_d, C_GA); prefetch(wpm_d, 0); prefetch(win_d, C_GM)
            P.barrier()
            if dbg and b == 0:
                fresh("dbg")
                for h in range(NH):
                    for g in range(N5):
                        gs = slice(g * 512, (g + 1) * 512)
                        P.op("dve", lambda e, h=h, gs=gs: e.tensor_copy(out=scrF[:, 0:512], in_=yaT[:, h, gs]), reads=[b_ya[h]], writes=[B.dbg])
                        P.dma("sp", lambda e, h=h, gs=gs: e.dma_start(out=dya_d[:, h, gs], in_=scrF[:, 0:512]), "dbg", reads=[B.dbg], writes=[b_out])
                        P.op("dve", lambda e, h=h, gs=gs: e.tensor_copy(out=scrF[:, 0:512], in_=ymT[:, h, gs]), reads=[b_ym[h]], writes=[B.dbg])
                        P.dma("sp", lambda e, h=h, gs=gs: e.dma_start(out=dym_d[:, h, gs], in_=scrF[:, 0:512]), "dbg", reads=[B.dbg], writes=[b_out])
                P.barrier()
            fresh("mg", "wout", "sga", "sgm", "t1", "t2")
            for half in range(2):
                P.dma("pool", lambda e, half=half: e.dma_start(out=woutb[:, half * 4:(half + 1) * 4, :],
                                                              in_=wout_d[half * 512:(half + 1) * 512, :].rearrange("(c p) n -> p c n", p=128)),
                      "wout", writes=[B.wout])
            for jc in range(8):
                wpa_t, wpa_b = get_w(wpa_d, jc * 128)
                wga_t, wga_b = get_w(win_d, C_GA + jc * 128)
                wpm_t, wpm_b = get_w(wpm_d, jc * 128)
                wgm_t, wgm_b = get_w(win_d, C_GM + jc * 128)
                for tt in range(N5):
                    tsl = slice(tt * 512, (tt + 1) * 512)

                    def mmgroup(bk, wt_, wb_, rhsT, rbufs, tsl):
                        fns = [(lambda e, c=c: e.matmul(bank[bk][:, 0:512], lhsT=wt_[:, c, :], rhs=rhsT[:, c, tsl], start=(c == 0), stop=(c == 7))) for c in range(8)]
                        P.group("pe", fns, reads=[wb_] + rbufs, writes=[b_bank[bk]])

                    mmgroup(2, wga_t, wga_b, xnT, [b_xnT], tsl)
                    mmgroup(3, wpa_t, wpa_b, yaT, b_ya, tsl)
                    mmgroup(4, wgm_t, wgm_b, xnT, [b_xnT], tsl)
                    mmgroup(5, wpm_t, wpm_b, ymT, b_ym, tsl)
                    P.op("act", lambda e, jc=jc: e.activation(out=sga, in_=bank[2][:, 0:512], func=AF.Sigmoid, bias=cst[:, K_BG + jc:K_BG + jc + 1], scale=1.0),
                         reads=[b_bank[2], b_cst], writes=[B.sga])
                    P.op("act", lambda e, jc=jc: e.activation(out=sgm, in_=bank[4][:, 0:512], func=AF.Sigmoid, bias=cst[:, K_BG + 8 + jc:K_BG + 8 + jc + 1], scale=1.0),
                         reads=[b_bank[4], b_cst], writes=[B.sgm])
                    P.op("dve", lambda e: e.tensor_tensor(out=t1, in0=bank[3][:, 0:512], in1=sga, op=ALU.mult), reads=[b_bank[3], B.sga], writes=[B.t1])
                    P.op("dve", lambda e: e.tensor_tensor(out=t2, in0=bank[5][:, 0:512], in1=sgm, op=ALU.mult), reads=[b_bank[5], B.sgm], writes=[B.t2])
                    P.op("dve", lambda e, jc=jc, tsl=tsl: e.tensor_tensor(out=mgT[:, jc, tsl], in0=t1, in1=t2, op=ALU.add), reads=[B.t1, B.t2], writes=[B.mg])
            for t in range(NT):
                i = t % NXT
                ts_ = slice(t * 128, (t + 1) * 128)
                P.dma("sp", lambda e, i=i, ts_=ts_, b=b: e.dma_start(out=xt[i][:], in_=x_d[b, ts_, :]), ("x", i), writes=[b_xt[i]])
                for nb in range(2):
                    bk = next_pb()
                    fns = [(lambda e, c=c, bk=bk, nb=nb, ts_=ts_: e.matmul(bank[bk][:, 0:512], lhsT=mgT[:, c, ts_], rhs=woutb[:, c, nb * 512:(nb + 1) * 512],
                                                                        start=(c == 0), stop=(c == 7))) for c in range(8)]
                    P.group("pe", fns, reads=[B.mg, B.wout], writes=[b_bank[bk]])
                    P.op("dve", lambda e, i=i, bk=bk, nb=nb: e.tensor_tensor(out=xt[i][:, nb * 512:(nb + 1) * 512], in0=bank[bk][:, 0:512],
                                                                          in1=xt[i][:, nb * 512:(nb + 1) * 512], op=ALU.add),
                         reads=[b_bank[bk], b_xt[i]], writes=[b_xt[i]])
                P.op("act", lambda e, i=i: e.activation(out=xs[i][:], in_=xt[i][:], func=AF.Square, accum_out=sm[:, 2:3]),
                     reads=[b_xt[i]], writes=[b_xs[i], b_sm])
                rsqrt_act(sm[:, 3:4], sm[:, 2:3], 1.0 / D, NORM_EPS, [b_sm], [b_sm])
                P.op("dve", lambda e, i=i: e.scalar_tensor_tensor(out=xt[i][:], in0=xt[i][:], scalar=sm[:, 3:4], in1=nfb[:], op0=ALU.mult, op1=ALU.mult),
                     reads=[b_xt[i], b_sm, b_cst], writes=[b_xt[i]])
                P.dma("pool", lambda e, i=i, ts_=ts_, b=b: e.dma_start(out=out_d[b, ts_, :], in_=xt[i][:]), "out", reads=[b_xt[i]], writes=[b_out])
            P.barrier()

        P.barrier()
        P.wait_all("sp", [b_out] + b_xt)
        P.emit()
    return nc


def _rel_bucket_np(rel):
    nb = 16
    me = 8
    bucket = np.where(rel > 0, nb, 0)
    n = np.abs(rel)
    nf = np.maximum(n, 1).astype(np.float32)
    large = me + (np.log(nf / np.float32(me)) / np.float32(math.log(128 / me)) * np.float32(nb - me)).astype(np.int32)
    large = np.minimum(large, nb - 1)
    return bucket + np.where(n < me, n, large)


def _host_consts(S, norm_w, lam, subln_w, rel_bias, conv_w, conv_b, b_if, mh_w, b_gate, norm_final):
    f = np.float32
    NT = S // 128
    cst = np.zeros((128, K_TOT), f)
    cst[:, K_NW:K_NW + 8] = norm_w[0].reshape(8, 128).T
    cst[:, K_LAM:K_LAM + 256] = np.broadcast_to(lam[0].reshape(1, 256), (128, 256))
    cst[:, K_SUB] = subln_w[0]
    cst[:, K_CH:K_CH + 8] = np.broadcast_to(rel_bias[15:16, :], (128, 8))
    cw = conv_w[0].reshape(2, 4, 8, 128)
    cst[:, K_CW:K_CW + 64] = cw.transpose(3, 0, 1, 2).reshape(128, 64)
    cb = conv_b[0].reshape(2, 8, 128)
    cst[:, K_CB:K_CB + 16] = cb.transpose(2, 0, 1).reshape(128, 16)
    cst[:, K_MHW:K_MHW + 8] = mh_w[0].reshape(8, 128).T
    bg = b_gate[0].reshape(2, 8, 128)
    cst[:, K_BG:K_BG + 16] = bg.transpose(2, 0, 1).reshape(128, 16)
    bif = np.broadcast_to(b_if[0].reshape(1, 1, 16), (128, NT, 16)).reshape(128, NT * 16).astype(f)
    nfb = np.broadcast_to(norm_final.reshape(1, D), (128, D)).astype(f)
    kk = np.arange(128)[:, None]
    qq = np.arange(256)[None, :]
    biasg = np.zeros((NH, 128, 768), f)
    maskc = np.zeros((128, 768), f)
    for o in range(3):
        krel = (o - 1) * 128 + kk
        rel = krel - qq
        idx = _rel_bucket_np(rel)
        g = rel_bias[idx, :]
        biasg[:, :, o * 256:(o + 1) * 256] = g.transpose(2, 0, 1)
        allowed = (np.floor_divide(krel, 64) <= np.floor_divide(qq, 64))
        maskc[:, o * 256:(o + 1) * 256] = np.where(allowed, 0.0, -1e30)
    return dict(cst=cst, bif=np.ascontiguousarray(bif), nfb=np.ascontiguousarray(nfb), biasg=biasg, maskc=maskc)


_NC_CACHE = {}


def kernel(x, norm_w, w_in, lam, subln_w, rel_bias, conv_w, conv_b, b_if, mh_w, b_gate, w_pa, w_pm, w_out, norm_final):
    x = np.asarray(x, np.float32)
    Bt, S, _ = x.shape
    nseq = Bt // NCORES
    args = [np.asarray(a, np.float32) for a in (norm_w, lam, subln_w, rel_bias, conv_w, conv_b, b_if, mh_w, b_gate, norm_final)]
    consts = _host_consts(S, *args)
    key = (nseq, S)
    if key not in _NC_CACHE:
        _NC_CACHE[key] = build(nseq, S)
    nc = _NC_CACHE[key]
    shared = dict(w_in=np.ascontiguousarray(np.asarray(w_in, np.float32)[0]),
                  w_pa=np.ascontiguousarray(np.asarray(w_pa, np.float32)[0]),
                  w_pm=np.ascontiguousarray(np.asarray(w_pm, np.float32)[0]),
                  w_out=np.ascontiguousarray(np.asarray(w_out, np.float32)[0]), **consts)
    in_maps = []
    for c in range(NCORES):
        m = dict(shared)
        m["x"] = np.ascontiguousarray(x[c * nseq:(c + 1) * nseq])
        in_maps.append(m)
    res = run_bass_kernel_spmd(nc, in_maps, core_ids=list(range(NCORES)))
    out = np.concatenate([np.asarray(r["out"], np.float32) for r in res.results], axis=0)
    return out
```
